# Optimizing a Trainium2 kernel written in Bass

```python
import jax, jax.numpy as jnp
from jax import lax
import numpy as np

D_MODEL = 1024
BATCH = 8
SEQ = 2048
DEPTH = 2
DEC_BATCH = 128
DEC_SEQ = 1
PAST_LEN = 16384
PAGE_SIZE = 128

RET_HEADS = 4
RET_WIDTH = D_MODEL // 2
RET_DV = RET_WIDTH // RET_HEADS
RET_DK = RET_DV // 2
CONV_A_WIDTH = D_MODEL // 4
CONV_A_K = 31
CONV_C_WIDTH = D_MODEL - RET_WIDTH - CONV_A_WIDTH
CONV_C_K = 3
MIX_WIDTH = RET_WIDTH + CONV_A_WIDTH + CONV_C_WIDTH
IN_SPLITS = (RET_HEADS * RET_DK, RET_HEADS * RET_DK, RET_WIDTH, RET_WIDTH,
             CONV_A_WIDTH, CONV_A_WIDTH, CONV_C_WIDTH, CONV_C_WIDTH, CONV_C_WIDTH)
IN_WIDTH = sum(IN_SPLITS)
D_FF = ((8 * D_MODEL // 3 + 255) // 256) * 256
RET_CHUNK = 128
ROPE_BASE = 10000.0
EPS = 1e-6

kernel_name = "hybrid_retention_conformer_shortconv_decoder_step"


def _rmsnorm(x, g):
    xf = x.astype(jnp.float32)
    y = xf * lax.rsqrt(jnp.mean(xf * xf, axis=-1, keepdims=True) + EPS)
    return (y * g.astype(jnp.float32)).astype(x.dtype)


def _layernorm(x, g, b):
    xf = x.astype(jnp.float32)
    mu = jnp.mean(xf, axis=-1, keepdims=True)
    var = jnp.mean(jnp.square(xf - mu), axis=-1, keepdims=True)
    y = (xf - mu) * lax.rsqrt(var + EPS)
    return (y * g.astype(jnp.float32) + b.astype(jnp.float32)).astype(x.dtype)


def _rope(x, pos):
    half = x.shape[-1] // 2
    inv = ROPE_BASE ** (-jnp.arange(half, dtype=jnp.float32) / half)
    ang = pos[:, None] * inv[None, :]
    cos = jnp.cos(ang)[None, :, None, :]
    sin = jnp.sin(ang)[None, :, None, :]
    xf = x.astype(jnp.float32)
    x1, x2 = xf[..., :half], xf[..., half:]
    return jnp.concatenate([x1 * cos - x2 * sin, x1 * sin + x2 * cos], axis=-1).astype(x.dtype)


def _retention(q, k, v, s0, chunk):
    B, L, H, DK = q.shape
    DV = v.shape[-1]
    nc = L // chunk
    dt = q.dtype
    log_g = jnp.log(1.0 - jnp.exp2(-5.0 - jnp.arange(H, dtype=jnp.float32)))
    idx = jnp.arange(chunk, dtype=jnp.float32)
    diff = idx[:, None] - idx[None, :]
    dmat = jnp.where(diff[None] >= 0, jnp.exp(jnp.maximum(diff, 0.0)[None] * log_g[:, None, None]), 0.0).astype(dt)
    read_dec = jnp.exp((idx + 1.0)[:, None] * log_g[None, :]).astype(dt)
    upd_dec = jnp.exp((chunk - 1.0 - idx)[:, None] * log_g[None, :]).astype(dt)
    chunk_dec = jnp.exp(chunk * log_g).astype(dt)

    def to_chunks(t):
        return jnp.moveaxis(t.reshape(B, nc, chunk, H, t.shape[-1]), 1, 0)

    def step(s, inp):
        qi, ki, vi = inp
        scores = jnp.einsum('bnhd,bmhd->bhnm', qi, ki) * dmat[None]
        o = (jnp.einsum('bhnm,bmhe->bnhe', scores, vi)
             + jnp.einsum('bnhd,bhde->bnhe', qi, s) * read_dec[None, :, :, None])
        s = (s * chunk_dec[None, :, None, None]
             + jnp.einsum('bmhd,bmhe->bhde', ki * upd_dec[None, :, :, None], vi))
        return s, o

    s_new, o = lax.scan(step, s0, (to_chunks(q), to_chunks(k), to_chunks(v)))
    o = jnp.moveaxis(o, 0, 1).reshape(B, L, H, DV)
    return o, s_new


def _causal_dwconv(x, buf, w):
    K, C = w.shape
    xp = jnp.concatenate([buf, x], axis=1)
    y = lax.conv_general_dilated(xp, w[:, None, :], window_strides=(1,), padding='VALID',
                                 dimension_numbers=('NWC', 'WIO', 'NWC'), feature_group_count=C)
    return y, xp[:, -(K - 1):, :]


def _layer(x, c, s_ret, buf_a, buf_c, pos,
           ada_w, ada_b, g_pre_mix, g_post_mix, g_pre_ffn, g_post_ffn,
           w_in, w_out, ret_gn_g, conv_a_w, conv_a_b, conv_a_ln_g, conv_a_ln_b,
           conv_c_w, ffn_w1, ffn_w3, ffn_w2):
    B, L, _ = x.shape
    mod = jax.nn.silu(c) @ ada_w + ada_b
    sh_m, sc_m, gt_m, sh_f, sc_f, gt_f = [m[:, None, :] for m in jnp.split(mod, 6, axis=-1)]

    h = _rmsnorm(x, g_pre_mix) * (1.0 + sc_m) + sh_m
    proj = h @ w_in
    bounds = np.cumsum(IN_SPLITS)[:-1].tolist()
    q, k, v, g, a_val, a_gate, cb, cc, cx = jnp.split(proj, bounds, axis=-1)

    q = _rope(q.reshape(B, L, RET_HEADS, RET_DK), pos)
    k = _rope(k.reshape(B, L, RET_HEADS, RET_DK), pos) * (RET_DK ** -0.5)
    v = v.reshape(B, L, RET_HEADS, RET_DV)
    chunk = RET_CHUNK if L % RET_CHUNK == 0 else L
    o_ret, s_ret_new = _retention(q, k, v, s_ret, chunk)
    o_ret = _layernorm(o_ret, ret_gn_g.reshape(RET_HEADS, RET_DV), jnp.zeros_like(ret_gn_g).reshape(RET_HEADS, RET_DV))
    o_ret = o_ret.reshape(B, L, RET_WIDTH) * jax.nn.silu(g)

    u = a_val * jax.nn.sigmoid(a_gate)
    ua, buf_a_new = _causal_dwconv(u, buf_a, conv_a_w)
    o_a = jax.nn.silu(_layernorm(ua + conv_a_b, conv_a_ln_g, conv_a_ln_b))

    z = cc * cx
    zc, buf_c_new = _causal_dwconv(z, buf_c, conv_c_w)
    o_c = cb * zc

    mix = jnp.concatenate([o_ret, o_a, o_c], axis=-1) @ w_out
    x = x + gt_m * _rmsnorm(mix, g_post_mix)

    h = _rmsnorm(x, g_pre_ffn) * (1.0 + sc_f) + sh_f
    f = (jax.nn.silu(h @ ffn_w1) * (h @ ffn_w3)) @ ffn_w2
    x = x + gt_f * _rmsnorm(f, g_post_ffn)
    return x, s_ret_new, buf_a_new, buf_c_new


def setup_inputs(seed: int = 0) -> dict:
    key = jax.random.key(seed)
    ks = jax.random.split(key, 24)
    f32 = jnp.float32
    n = lambda k, shape, s: (jax.random.normal(k, shape, f32) * s)
    return {
        "x_prompt": n(ks[0], (BATCH, SEQ, D_MODEL), 1.0),
        "x_sample": n(ks[1], (DEC_BATCH, DEC_SEQ, D_MODEL), 1.0),
        "c_prompt": n(ks[2], (BATCH, D_MODEL), 1.0),
        "c_sample": n(ks[3], (DEC_BATCH, D_MODEL), 1.0),
        "state_ret": n(ks[4], (DEPTH, DEC_BATCH, RET_HEADS, RET_DK, RET_DV), 0.5),
        "state_conv_a": n(ks[5], (DEPTH, DEC_BATCH, CONV_A_K - 1, CONV_A_WIDTH), 0.5),
        "state_conv_c": n(ks[6], (DEPTH, DEC_BATCH, CONV_C_K - 1, CONV_C_WIDTH), 0.5),
        "ada_w": n(ks[7], (DEPTH, D_MODEL, 6 * D_MODEL), 0.5 * D_MODEL ** -0.5),
        "ada_b": n(ks[8], (DEPTH, 6 * D_MODEL), 0.01),
        "norm_pre_mix": 1.0 + n(ks[9], (DEPTH, D_MODEL), 0.05),
        "norm_post_mix": 1.0 + n(ks[10], (DEPTH, D_MODEL), 0.05),
        "norm_pre_ffn": 1.0 + n(ks[11], (DEPTH, D_MODEL), 0.05),
        "norm_post_ffn": 1.0 + n(ks[12], (DEPTH, D_MODEL), 0.05),
        "w_in": n(ks[13], (DEPTH, D_MODEL, IN_WIDTH), D_MODEL ** -0.5),
        "w_out": n(ks[14], (DEPTH, MIX_WIDTH, D_MODEL), MIX_WIDTH ** -0.5),
        "ret_gn_g": 1.0 + n(ks[15], (DEPTH, RET_WIDTH), 0.05),
        "conv_a_w": n(ks[16], (DEPTH, CONV_A_K, CONV_A_WIDTH), CONV_A_K ** -0.5),
        "conv_a_b": n(ks[17], (DEPTH, CONV_A_WIDTH), 0.01),
        "conv_a_ln_g": 1.0 + n(ks[18], (DEPTH, CONV_A_WIDTH), 0.05),
        "conv_a_ln_b": n(ks[19], (DEPTH, CONV_A_WIDTH), 0.01),
        "conv_c_w": n(ks[20], (DEPTH, CONV_C_K, CONV_C_WIDTH), CONV_C_K ** -0.5),
        "ffn_w1": n(ks[21], (DEPTH, D_MODEL, D_FF), D_MODEL ** -0.5),
        "ffn_w3": n(ks[22], (DEPTH, D_MODEL, D_FF), D_MODEL ** -0.5),
        "ffn_w2": n(ks[23], (DEPTH, D_FF, D_MODEL), D_FF ** -0.5),
    }


def reference(x_prompt, x_sample, c_prompt, c_sample, state_ret, state_conv_a, state_conv_c,
              ada_w, ada_b, norm_pre_mix, norm_post_mix, norm_pre_ffn, norm_post_ffn,
              w_in, w_out, ret_gn_g, conv_a_w, conv_a_b, conv_a_ln_g, conv_a_ln_b,
              conv_c_w, ffn_w1, ffn_w3, ffn_w2):
    Bp, Lp, _ = x_prompt.shape
    Ls = x_sample.shape[1]
    dt = x_prompt.dtype
    pos_p = jnp.arange(Lp, dtype=jnp.float32)
    pos_s = PAST_LEN + jnp.arange(Ls, dtype=jnp.float32)
    yp, ys = x_prompt, x_sample
    ret_p, ca_p, cc_p, ret_s, ca_s, cc_s = [], [], [], [], [], []
    for l in range(DEPTH):
        params = (ada_w[l], ada_b[l], norm_pre_mix[l], norm_post_mix[l], norm_pre_ffn[l],
                  norm_post_ffn[l], w_in[l], w_out[l], ret_gn_g[l], conv_a_w[l], conv_a_b[l],
                  conv_a_ln_g[l], conv_a_ln_b[l], conv_c_w[l], ffn_w1[l], ffn_w3[l], ffn_w2[l])
        s0 = jnp.zeros((Bp, RET_HEADS, RET_DK, RET_DV), dt)
        ba0 = jnp.zeros((Bp, CONV_A_K - 1, CONV_A_WIDTH), dt)
        bc0 = jnp.zeros((Bp, CONV_C_K - 1, CONV_C_WIDTH), dt)
        yp, sr, ba, bc = _layer(yp, c_prompt, s0, ba0, bc0, pos_p, *params)
        ret_p.append(sr); ca_p.append(ba); cc_p.append(bc)
        ys, sr, ba, bc = _layer(ys, c_sample, state_ret[l], state_conv_a[l], state_conv_c[l], pos_s, *params)
        ret_s.append(sr); ca_s.append(ba); cc_s.append(bc)
    return (yp, ys, jnp.stack(ret_p), jnp.stack(ca_p), jnp.stack(cc_p),
            jnp.stack(ret_s), jnp.stack(ca_s), jnp.stack(cc_s))
```

```python
import numpy as np
from contextlib import ExitStack
import concourse.bass as bass
import concourse.mybir as mybir
from concourse.bass_utils import run_bass_kernel_spmd

F32 = mybir.dt.float32
BF16 = mybir.dt.bfloat16
AF = mybir.ActivationFunctionType
ALU = mybir.AluOpType
AX = mybir.AxisListType

D = 1024
KC = 8
T = 2048
DEPTH = 2
NS = 16
DFF = 2816
FC = 22
INW = 2816
EPS = 1e-6
T1 = 256
NT1 = T // T1
T2 = 512
NH = 1024
RING = 2
SLOT = 4096

C_ID = 0
C_COS = 128
C_SIN = C_COS + 512
C_DMT = C_SIN + 512
C_RDQ = C_DMT + 512
C_UPD = C_RDQ + 4
C_GS = C_UPD + 4
C_SCOS = C_GS + 1
C_SSIN = C_SCOS + 32
C_EPS = C_SSIN + 32
NCC = C_EPS + 1


def _host_consts():
    c = np.zeros((128, NCC), np.float32)
    c[:, C_ID:C_ID + 128] = np.eye(128, dtype=np.float32)
    half = 32
    inv = (np.float32(10000.0) ** (-(np.arange(half, dtype=np.float32) / np.float32(half)))).astype(np.float32)
    pos = np.arange(T, dtype=np.float32)
    ang = (pos[:, None] * inv[None, :]).astype(np.float32)
    cos = np.cos(ang.astype(np.float64)).astype(np.float32)
    sin = np.sin(ang.astype(np.float64)).astype(np.float32)
    c[:, C_COS:C_COS + 512] = cos.reshape(16, 128, 32).transpose(1, 0, 2).reshape(128, 512)
    c[:, C_SIN:C_SIN + 512] = sin.reshape(16, 128, 32).transpose(1, 0, 2).reshape(128, 512)
    angs = (np.float32(16384.0) * inv).astype(np.float32)
    c[:, C_SCOS:C_SCOS + 32] = np.cos(angs.astype(np.float64)).astype(np.float32)[None, :]
    c[:, C_SSIN:C_SSIN + 32] = np.sin(angs.astype(np.float64)).astype(np.float32)[None, :]
    hh = np.arange(4, dtype=np.float64)
    g = 1.0 - np.exp2(-5.0 - hh)
    lg = np.log(g)
    idx = np.arange(128, dtype=np.float64)
    m = idx[:, None, None]
    n = idx[None, None, :]
    dm = np.where(n >= m, np.exp((-m - 1.0) * lg[None, :, None]), 0.0)
    c[:, C_DMT:C_DMT + 512] = dm.reshape(128, 512).astype(np.float32)
    c[:, C_RDQ:C_RDQ + 4] = np.exp((idx[:, None] + 1.0) * lg[None, :]).astype(np.float32)
    c[:, C_UPD:C_UPD + 4] = (0.125 * np.exp((127.0 - idx[:, None]) * lg[None, :])).astype(np.float32)
    pidx = np.arange(128)
    c[:, C_GS] = g[(pidx // 2) % 4].astype(np.float32)
    c[:, C_EPS] = np.float32(EPS)
    chunk_dec = np.exp(128.0 * lg)
    return c, [float(x) for x in chunk_dec]


class Prog:
    ENG = ("pe", "dve", "act", "pool", "sp")

    def __init__(self):
        self.cnt = {e: 0 for e in self.ENG}
        self.dcnt = {}
        self.seen = {e: {} for e in self.ENG}
        self.lastw = {}
        self.reads = {}
        self.plan = {e: [] for e in self.ENG}

    def _waits(self, e, r, w):
        need = {}
        for k in r:
            t = self.lastw.get(k)
            if t:
                need[t[0]] = max(need.get(t[0], 0), t[1])
        for k in w:
            t = self.lastw.get(k)
            if t:
                need[t[0]] = max(need.get(t[0], 0), t[1])
            for nm, v in self.reads.get(k, {}).items():
                need[nm] = max(need.get(nm, 0), v)
        out = []
        for nm, v in need.items():
            if e == "pe" and nm == "s_pe":
                continue
            if self.seen[e].get(nm, 0) < v:
                self.seen[e][nm] = v
                out.append((nm, v))
        return out

    def _record(self, tok, r, w):
        for k in r:
            d = self.reads.setdefault(k, {})
            d[tok[0]] = max(d.get(tok[0], 0), tok[1])
        for k in w:
            self.lastw[k] = tok
            self.reads[k] = {}

    def op(self, e, fn, r=(), w=(), inc=True):
        pr = [k for k in r if k.startswith("ps")]
        if pr:
            r = [k for k in r if not k.startswith("ps")]
            w = list(w) + pr
        waits = self._waits(e, r, w)
        nm = "s_" + e
        if inc:
            self.cnt[e] += 1
            tok = (nm, self.cnt[e])
        else:
            tok = (nm, self.cnt[e] + 1)
        self._record(tok, r, w)
        self.plan[e].append((waits, fn, (nm, 1) if inc else None))

    def dma(self, q, chan, fn, r=(), w=()):
        waits = self._waits(q, r, w)
        nm = "d_" + chan
        self.dcnt[nm] = self.dcnt.get(nm, 0) + 16
        tok = (nm, self.dcnt[nm])
        self._record(tok, r, w)
        self.plan[q].append((waits, fn, (nm, 16)))

    def _all_tokens(self):
        toks = [("s_" + e, self.cnt[e]) for e in self.ENG if self.cnt[e] > 0]
        toks += list(self.dcnt.items())
        return toks

    def barrier(self, skip=()):
        toks = [t for t in self._all_tokens() if not t[0].startswith(tuple(skip))] if skip else self._all_tokens()
        for e in self.ENG:
            waits = []
            for nm, v in toks:
                if self.seen[e].get(nm, 0) < v:
                    self.seen[e][nm] = v
                    waits.append((nm, v))
            if waits:
                self.plan[e].append((waits, None, None))
        if skip:
            sk = tuple(skip)
            self.lastw = {k: v for k, v in self.lastw.items() if v[0].startswith(sk)}
            self.reads = {k: {n: c for n, c in d.items() if n.startswith(sk)} for k, d in self.reads.items()}
            self.reads = {k: d for k, d in self.reads.items() if d}
        else:
            self.lastw = {}
            self.reads = {}

    def emit(self, nc, es):
        toks = self._all_tokens()
        sems = {}
        for e in self.ENG:
            sems["s_" + e] = es.enter_context(nc.semaphore("s_" + e))
        for nm in self.dcnt:
            sems[nm] = es.enter_context(nc.semaphore(nm))
        blk = es.enter_context(nc.Block())
        engs = {"pe": nc.tensor, "dve": nc.vector, "act": nc.scalar, "pool": nc.gpsimd, "sp": nc.sync}

        def run(e):
            h = engs[e]
            for waits, fn, inc in self.plan[e]:
                for nm, v in waits:
                    h.wait_ge(sems[nm], v)
                if fn is not None:
                    ins = fn()
                    if inc is not None:
                        ins.then_inc(sems[inc[0]], inc[1])
            if e == "sp":
                for nm, v in toks:
                    h.wait_ge(sems[nm], v)

        @blk.sync
        def _(x):
            run("sp")

        @blk.scalar
        def _(x):
            run("act")

        @blk.vector
        def _(x):
            run("dve")

        @blk.gpsimd
        def _(x):
            run("pool")

        @blk.tensor
        def _(x):
            run("pe")


def build_program(chunk_dec):
    nc = bass.Bass("TRN2", target_bir_lowering=False)
    P = Prog()

    def din(name, shape, dt=F32):
        return nc.dram_tensor(name, list(shape), dt, kind="ExternalInput").ap()

    def dout(name, shape, dt=F32):
        return nc.dram_tensor(name, list(shape), dt, kind="ExternalOutput").ap()

    def dscr(name, shape, dt=F32):
        return nc.dram_tensor(name, list(shape), dt, kind="Internal").ap()

    xp_d = din("xp", [T, D])
    xs_d = din("xs", [NS, D])
    c17_d = din("c17", [17, D])
    sret_d = din("sret", [DEPTH, NS, 4, 64, 128])
    sca_d = din("sca", [DEPTH, NS, 30, 256])
    scc_d = din("scc", [DEPTH, NS, 2, 256])
    adaw_d = din("ada_w", [DEPTH, D, 6 * D])
    win_d = din("w_in", [DEPTH, D, INW])
    wout_d = din("w_out", [DEPTH, D, D])
    w1_d = din("ffn_w1", [DEPTH, D, DFF])
    w3_d = din("ffn_w3", [DEPTH, D, DFF])
    w2_d = din("ffn_w2", [DEPTH, DFF, D])
    vecA_d = din("vecA", [DEPTH, 92, 128])
    vecB_d = din("vecB", [DEPTH, 62, 128])
    gn_d = din("gn", [DEPTH, 512])
    cst_d = din("consts", [128, NCC])

    yp_d = dout("yp", [T, D])
    ys_d = dout("ys", [NS, D])
    rp_d = dout("rp", [DEPTH, 4, 64, 128])
    cap_d = dout("cap", [DEPTH, 30, 256])
    ccp_d = dout("ccp", [DEPTH, 2, 256])
    rs_d = dout("rs", [DEPTH, NS, 4, 64, 128])
    cas_d = dout("cas", [DEPTH, NS, 30, 256])
    ccs_d = dout("ccs", [DEPTH, NS, 2, 256])

    scr_q = dscr("scr_q", [NS, 256])
    scr_k = dscr("scr_k", [NS, 256])
    scr_v = dscr("scr_v", [NS, 4, 2, 128])
    scr_o = dscr("scr_o", [128, 128])

    es = ExitStack()
    with es:
        def sb(name, shape, dt=F32):
            return es.enter_context(nc.sbuf_tensor(name, list(shape), dt))

        xT = sb("xT", [128, KC, NH])
        RB = sb("RB", [128, 30720], BF16)
        w_in = RB[:, 0:22528].rearrange("p (k n) -> p k n", k=KC)
        w_out = RB[:, 22528:30720].rearrange("p (k n) -> p k n", k=KC)
        hid = RB[:, 0:22528].rearrange("p (f n) -> p f n", f=FC)
        hT2 = RB[:, 22528:30720].rearrange("p (k n) -> p k n", k=KC)
        RCF = sb("RCF", [128, 9728])
        RCB = sb("RCB", [128, RING * SLOT], BF16)
        f_out = RCF[:, 0:8192].rearrange("p (k n) -> p k n", k=KC)
        p2tmp = [RCF[:, 8192 + i * 512: 8192 + (i + 1) * 512] for i in range(2)]
        rstd2 = RCF[:, 9216:9728]
        ring = [RCB[:, i * SLOT:(i + 1) * SLOT] for i in range(RING)]
        sq2 = RB[:, 0:4096].rearrange("p (k n) -> p k n", k=KC)
        o = 0

        def cf(n):
            nonlocal o
            v = RCF[:, o:o + n]
            o += n
            return v
        mo = cf(KC * T1).rearrange("p (k n) -> p k n", k=KC)
        u_ext = cf(2 * (30 + T1)).rearrange("p (c n) -> p c n", c=2)
        z_ext = cf(2 * (2 + T1)).rearrange("p (c n) -> p c n", c=2)
        cbf = cf(2 * T1).rearrange("p (c n) -> p c n", c=2)
        acc = cf(2 * T1).rearrange("p (c n) -> p c n", c=2)
        tcc = cf(2 * T1).rearrange("p (c n) -> p c n", c=2)
        rstd1 = cf(T1)
        tmpA = cf(T1)
        tmpB = cf(T1)
        tmpC = cf(T1)
        tmpD = cf(T1)
        rope_r = cf(512)
        rope_t = cf(512)
        sg = cf(512)
        on = cf(512)
        assert o <= 9728, o
        xstage = [RCF[:, 0:1024], RCF[:, 1024:2048]]
        ob = 0

        def cb_(n):
            nonlocal ob
            v = RCB[:, ob:ob + n]
            ob += n
            return v
        bufX = cb_(KC * T1).rearrange("p (k n) -> p k n", k=KC)
        bufY = cb_(KC * T1).rearrange("p (k n) -> p k n", k=KC)
        ybf = cb_(2 * T1).rearrange("p (c n) -> p c n", c=2)
        ysq = cb_(2 * T1).rearrange("p (c n) -> p c n", c=2)
        qt_bf = cb_(256)
        k_bf = cb_(256)
        kt_bf = cb_(256)
        v_bf = cb_(512)
        qkT = cb_(256)
        kTm = cb_(512)
        scT = cb_(512)
        oret = cb_(512)
        assert ob <= RING * SLOT, ob

        cst = sb("cst", [128, NCC])
        ident = cst[:, C_ID:C_ID + 128]
        ident_bf = sb("ident_bf", [128, 128], BF16)
        ones_d = sb("ones_d", [128, 128], BF16)
        ones_c = sb("ones_c", [128, 128], BF16)
        colA = sb("colA", [128, DEPTH, 92])
        colB = sb("colB", [128, DEPTH, 62])
        gn_bc = sb("gn_bc", [128, 1, 512])
        vst = sb("vst", [128, 128])
        modT_l = [sb("modT%d" % i, [128, 48, 17]) for i in range(DEPTH)]
        tabA_m_l = [sb("tabA_m%d" % i, [128, KC, 17]) for i in range(DEPTH)]
        tabG_m_l = [sb("tabG_m%d" % i, [128, KC, 17]) for i in range(DEPTH)]
        tabA_f_l = [sb("tabA_f%d" % i, [128, KC, 17]) for i in range(DEPTH)]
        tabG_f_l = [sb("tabG_f%d" % i, [128, KC, 17]) for i in range(DEPTH)]
        TB = {}

        def set_layer(l):
            TB["modT"] = modT_l[l]
            TB["A_m"] = tabA_m_l[l]
            TB["G_m"] = tabG_m_l[l]
            TB["A_f"] = tabA_f_l[l]
            TB["G_f"] = tabG_f_l[l]
            TB["B_m"] = modT_l[l][:, 0:8, :]
            TB["B_f"] = modT_l[l][:, 24:32, :]
            TB["Sst"] = Sst_l[l]
            TB["S_bf"] = S_bf_l[l]
        scT17 = sb("scT17", [128, KC, 17], BF16)
        Sst_l = [sb("Sst%d" % i, [128, 2, 128]) for i in range(DEPTH)]
        S_bf_l = [[sb("S_bf%d_%d" % (i, q), [128, 4, 128], BF16) for q in range(2)] for i in range(DEPTH)]
        bufX2 = sb("bufX2", [128, KC, T1], BF16)
        sqA = sb("sqA", [128, KC, T1], BF16)
        pnf = sb("pnf", [128, 3 * T1])
        save_u = sb("save_u", [128, DEPTH, 2, 30], BF16)
        diagW = sb("diagW", [128, 62, 128], BF16)
        u_bf = sb("u_bf", [128, 2, 30 + T1], BF16)
        save_z = sb("save_z", [128, DEPTH, 2, 2])
        lnst = sb("lnst", [128, 4, 6])
        lnmv = sb("lnmv", [128, 4, 2])
        lnr = sb("lnr", [128, 4])
        lnb = sb("lnb", [128, 4])
        st30 = sb("st30", [32, 256])
        xsT = sb("xsT", [128, KC, NS])
        hTs = sb("hTs", [128, KC, NS], BF16)
        sqs = sb("sqs", [128, KC, NS], BF16)
        mixTs = sb("mixTs", [128, KC, NS], BF16)
        mos = sb("mos", [128, KC, NS])
        hT2s = sb("hT2s", [128, KC, NS], BF16)
        hids = sb("hids", [128, FC, NS], BF16)
        fouts = sb("fouts", [128, KC, NS])
        sq2s = sb("sq2s", [128, KC, NS], BF16)
        smallf = sb("smallf", [128, 12 * NS])
        u_s = smallf[:, 0:32].rearrange("p (c n) -> p c n", c=2)
        z_s = smallf[:, 32:64].rearrange("p (c n) -> p c n", c=2)
        cb_s = smallf[:, 64:96].rearrange("p (c n) -> p c n", c=2)
        acc_s = smallf[:, 96:128].rearrange("p (c n) -> p c n", c=2)
        tcc_s = smallf[:, 128:160].rearrange("p (c n) -> p c n", c=2)
        rstd_s = smallf[:, 160:176]
        tmps = [smallf[:, 176 + i * 16: 192 + i * 16] for i in range(1)]
        ybf_s = sb("ybf_s", [128, 2, NS], BF16)
        ysq_s = sb("ysq_s", [128, 2, NS], BF16)

        psf = [es.enter_context(nc.psum_tensor("ps%d" % i, [128, 512], F32)) for i in range(7)]
        psb = es.enter_context(nc.psum_tensor("psb", [128, 1024], BF16))

        def mm(out, lhsT, rhs, start, stop, r, w, inc=False):
            P.op("pe", lambda: nc.tensor.matmul(out, lhsT, rhs, start=start, stop=stop), r, w, inc)

        def tr(out, in_, idn, r, w):
            P.op("pe", lambda: nc.tensor.transpose(out, in_, idn), r, w, True)

        def act(out, in_, func, r, w, bias=None, scale=None):
            kw = {}
            if bias is not None:
                kw["bias"] = bias
            if scale is not None:
                kw["scale"] = scale
            P.op("act", lambda: nc.scalar.activation(out, in_, func, **kw), r, w)

        def veng(e):
            return nc.vector if e == "dve" else nc.gpsimd

        def tt(e, out, in0, in1, op, r, w):
            P.op(e, lambda: veng(e).tensor_tensor(out, in0, in1, op), r, w)

        def ts(e, out, in0, s1, s2, op0, op1, r, w):
            if s2 is None:
                P.op(e, lambda: veng(e).tensor_single_scalar(out, in0, s1, op0), r, w)
            else:
                P.op(e, lambda: veng(e).tensor_scalar(out, in0, s1, s2, op0, op1), r, w)

        def stt(e, out, in0, sc, in1, op0, op1, r, w):
            P.op(e, lambda: veng(e).scalar_tensor_tensor(out, in0, sc, in1, op0, op1), r, w)

        def cp(e, out, in_, r, w):
            if e == "act":
                P.op("act", lambda: nc.scalar.copy(out, in_), r, w)
            else:
                P.op(e, lambda: veng(e).tensor_copy(out, in_), r, w)

        def mset(e, out, val, w):
            P.op(e, lambda: veng(e).memset(out, val), (), w)

        def rsqrt_eps(out, in_, M, r, wkey):
            act(out, in_, AF.Ln, list(r) + ["cst"], [wkey], bias=cst[0:M, C_EPS:C_EPS + 1])
            act(out, out, AF.Exp, [wkey], [wkey], scale=-0.5)

        def dma(q, chan, out, in_, r, w):
            eng = {"sp": nc.sync, "pool": nc.gpsimd, "act": nc.scalar}[q]
            P.dma(q, chan, lambda: eng.dma_start(out=out, in_=in_), r, w)

        dma("sp", "cst", cst[:], cst_d, (), ["cst"])
        cp("dve", ident_bf[:], ident, ["cst"], ["ident_bf"])
        mset("dve", ones_d[:], 1.0 / 1024.0, ["ones"])
        mset("dve", ones_c[:], 1.0 / 256.0, ["ones"])
        for l in range(DEPTH):
            dma("sp", "vst", vst[0:92, :], vecA_d[l], (), ["vst"])
            tr(psf[6][:, 0:92], vst[0:92, :], ident[0:92, 0:92], ["vst", "cst"], ["ps6"])
            cp("dve", colA[:, l, :], psf[6][:, 0:92], ["ps6"], ["colA"])
            dma("sp", "vst", vst[0:62, :], vecB_d[l], (), ["vst"])
            tr(psf[6][:, 0:62], vst[0:62, :], ident[0:62, 0:62], ["vst", "cst"], ["ps6"])
            cp("dve", colB[:, l, :], psf[6][:, 0:62], ["ps6"], ["colB"])
        c17 = RCF[0:17, 2048:3072]
        dma("sp", "c17", c17, c17_d, (), ["c17"])
        act(c17, c17, AF.Silu, ["c17"], ["c17"])
        for c in range(KC):
            tr(psf[6][:, c * 17:(c + 1) * 17], c17[0:17, c * 128:(c + 1) * 128], ident[0:17, 0:17],
               ["c17", "cst"], ["ps6"])
        cp("dve", scT17[:].rearrange("p k n -> p (k n)"), psf[6][:, 0:KC * 17], ["ps6"], ["scT17"])

        dma("sp", "xst0", xstage[0][0:NS, :], xs_d, (), ["xst0"])
        for c in range(KC):
            tr(psf[0][:, c * NS:(c + 1) * NS], xstage[0][0:NS, c * 128:(c + 1) * 128], ident[0:NS, 0:NS],
               ["xst0", "cst"], ["ps0"])
        cp("dve", xsT[:].rearrange("p k n -> p (k n)"), psf[0][:, 0:KC * NS], ["ps0"], ["xsT"])
        P.barrier()

        XK = ["xT"] + ["xT%d" % k for k in range(KC)]
        def stats_rstd(sqv, N, rstd_ap, ones, key_sq, key_rstd, bank=6):
            bk = "ps%d" % bank
            for k in range(KC):
                mm(psf[bank][:, 0:N], ones[:], sqv[:, k, :], k == 0, k == KC - 1, [key_sq, key_sq + "_%d" % k, "ones"], [bk],
                   inc=(k == KC - 1))
            rsqrt_eps(rstd_ap, psf[bank][:, 0:N], 128, [bk], key_rstd)

        def pre_norm(*a, **kw):
            for _ in pre_norm_g(*a, **kw):
                pass

        def pre_norm_g(xv_, N, sqv, key_sq, hv, key_h, rstd_ap, key_rstd, tmps_, key_tmp, A, B, col, is_tab, bank=6):
            for k in range(KC):
                act(sqv[:, k, :], xv_[:, k, :], AF.Square, XK, [key_sq + "_%d" % k])
                if k % 2 == 1:
                    yield
            stats_rstd(sqv, N, rstd_ap, ones_d, key_sq, key_rstd, bank)
            yield
            for k in range(KC):
                if k % 2 == 0 and k > 0:
                    yield
                tp = tmps_[k % len(tmps_)]
                tk = key_tmp + str(k % len(tmps_))
                tt("dve", tp, xv_[:, k, :], rstd_ap, ALU.mult, XK + [key_rstd], [tk])
                if not is_tab:
                    act(hv[:, k, :], tp, AF.Identity, [tk, "tab"], [key_h + "_%d" % k],
                        bias=B[:, k, col:col + 1], scale=A[:, k, col:col + 1])
                else:
                    tt("dve", tp, tp, A[:, k, col:col + N], ALU.mult, [tk, "tab"], [tk])
                    tt("dve", hv[:, k, :], tp, B[:, k, col:col + N], ALU.add, [tk, "tab"], [key_h])

        def post_norm_res(mov, key_mo, sqv, key_sq, N, rstd_ap, key_rstd, tmps_, key_tmp, G, col, is_tab, xv_):
            stats_rstd(sqv, N, rstd_ap, ones_d, key_sq, key_rstd)
            for k in range(KC):
                tp = tmps_[k % len(tmps_)]
                tk = key_tmp + str(k % len(tmps_))
                tt("dve", tp, mov[:, k, :], rstd_ap, ALU.mult, [key_mo, key_rstd], [tk])
                if not is_tab:
                    stt("dve", xv_[:, k, :], tp, G[:, k, col:col + 1], xv_[:, k, :], ALU.mult, ALU.add,
                        [tk, "tab", "xT%d" % k], ["xT%d" % k])
                else:
                    tt("dve", tp, tp, G[:, k, col:col + N], ALU.mult, [tk, "tab"], [tk])
                    tt("dve", xv_[:, k, :], xv_[:, k, :], tp, ALU.add, [tk, "xT%d" % k], ["xT%d" % k])

        def fm_proj(*a, **kw):
            for _ in fm_proj_g(*a, **kw):
                pass

        def fm_proj_g(l, hv, key_h, N, u_dst, z_dst, cb_dst, tcc_v, sig_tmp, key_sig, u_hook=None):
            order = [2, 3, 0, 1, 6, 7, 8, 9, 4, 5]
            for idx_, j in enumerate(order):
                if idx_ > 0:
                    yield
                pb = psf[idx_ % 2]
                pk = "ps%d" % (idx_ % 2)
                col0 = 1536 + j * 128
                for k in range(KC):
                    mm(pb[:, 0:N], w_in[:, k, col0:col0 + 128], hv[:, k, :], k == 0, k == KC - 1,
                       ["w_in%d" % (col0 // 512), key_h, key_h + "_%d" % k], [pk], inc=(k == KC - 1))
                ch = j % 2
                if j in (2, 3):
                    act(sig_tmp[ch], pb[:, 0:N], AF.Tanh, [pk], [key_sig + str(ch)], scale=0.5)
                    ts("dve", sig_tmp[ch], sig_tmp[ch], 0.5, 0.5, ALU.mult, ALU.add, [key_sig + str(ch)], [key_sig + str(ch)])
                elif j in (0, 1):
                    tt("dve", u_dst(ch), pb[:, 0:N], sig_tmp[ch], ALU.mult, [pk, key_sig + str(ch)], ["u_ext"])
                    if u_hook is not None:
                        u_hook(ch)
                elif j in (6, 7):
                    cp("act", tcc_v[:, ch, :], pb[:, 0:N], [pk], ["tcc"])
                elif j in (8, 9):
                    tt("dve", z_dst(ch), pb[:, 0:N], tcc_v[:, ch, :], ALU.mult, [pk, "tcc"], ["z_ext"])
                else:
                    cp("act", cb_dst[:, ch, :], pb[:, 0:N], [pk], ["cbf"])

        def tok_qkvg(l, hv_chunk, key_h, M, banks=(2, 3, 4)):
            for gi in range(3):
                pb = psf[banks[gi]]
                pk = "ps%d" % banks[gi]
                for k in range(KC):
                    mm(pb[0:M, :], hv_chunk(k), w_in[:, k, gi * 512:(gi + 1) * 512], k == 0, k == KC - 1,
                       [key_h, key_h + "_%d" % k, "w_in%d" % gi], [pk], inc=(k == KC - 1))

        def rope(M, cosv, sinv, dst, qb=2, rt=None, x=""):
            if rt is None:
                rt = rope_t
            src = psf[qb][0:M, :].rearrange("p (h d) -> p h d", h=8)
            dv = dst.rearrange("p (h d) -> p h d", h=8)
            tv = rt[0:M, :].rearrange("p (h d) -> p h d", h=8)
            cb3 = cosv.unsqueeze(1).to_broadcast([M, 8, 32])
            sb3 = sinv.unsqueeze(1).to_broadcast([M, 8, 32])
            pq = "ps%d" % qb
            kr, kt_ = "rope_r" + x, "rope_t" + x
            tt("dve", dv[:, :, 0:32], src[:, :, 0:32], cb3, ALU.mult, [pq, "cst"], [kr + "lo"])
            tt("dve", tv[:, :, 0:32], src[:, :, 32:64], sb3, ALU.mult, [pq, "cst"], [kt_ + "lo"])
            tt("dve", dv[:, :, 32:64], src[:, :, 0:32], sb3, ALU.mult, [pq, "cst"], [kr + "hi"])
            tt("dve", tv[:, :, 32:64], src[:, :, 32:64], cb3, ALU.mult, [pq, "cst"], [kt_ + "hi"])
            tt("dve", dv[:, :, 0:32], dv[:, :, 0:32], tv[:, :, 0:32], ALU.subtract, [kr + "lo", kt_ + "lo"], [kr + "lo", kr])
            tt("dve", dv[:, :, 32:64], dv[:, :, 32:64], tv[:, :, 32:64], ALU.add, [kr + "hi", kt_ + "hi"], [kr + "hi", kr])

        def ret_ln_gate(l, M, osrc, okeys, mix_dst, mix_key, BS=None):
            if BS is None:
                BS = dict(lnst=lnst, lnmv=lnmv, lnr=lnr, lnb=lnb, on=on, sg=sg, oret=oret, sfx="")
            return _ret_ln_gate(l, M, osrc, okeys, mix_dst, mix_key, BS)

        def _ret_ln_gate(l, M, osrc, okeys, mix_dst, mix_key, BS):
            lnst, lnmv, lnr, lnb, on, sg, oret, x = (BS["lnst"], BS["lnmv"], BS["lnr"], BS["lnb"], BS["on"], BS["sg"],
                                                     BS["oret"], BS["sfx"])
            ov = osrc.rearrange("p (h e) -> p h e", h=4)
            kon = "on" + x if x != "_1" else "rope_t_1"
            kor = "oret" + x if x != "_1" else "scT_1"
            for h in range(4):
                P.op("dve", (lambda h=h: nc.vector.bn_stats(lnst[0:M, h, :], ov[:, h, :])), okeys, ["lnst%d" % h + x])
            for h in range(4):
                P.op("dve", (lambda h=h: nc.vector.bn_aggr(lnmv[0:M, h, :], lnst[0:M, h, :])), ["lnst%d" % h + x],
                     ["lnmv%d" % h + x])
            yield
            lmk = ["lnmv%d" % h + x for h in range(4)]
            rsqrt_eps(lnr[0:M, :], lnmv[0:M, :, 1], M, lmk, "lnr" + x)
            stt("dve", lnb[0:M, :], lnmv[0:M, :, 0], -1.0, lnr[0:M, :], ALU.mult, ALU.mult, lmk + ["lnr" + x], ["lnb" + x])
            yield
            onv = on[0:M, :].rearrange("p (h e) -> p h e", h=4)
            for h in range(4):
                act(onv[:, h, :], ov[:, h, :], AF.Identity, okeys + ["lnr" + x, "lnb" + x], [kon + "_%d" % h],
                    bias=lnb[0:M, h:h + 1], scale=lnr[0:M, h:h + 1])
            yield
            tt("dve", oret[0:M, :], on[0:M, :], sg[0:M, :], ALU.mult, [kon] + [kon + "_%d" % h for h in range(4)] + ["sg" + x], [kor])
            yield
            for h in range(4):
                tr(psb[:, 512 + h * 128: 512 + h * 128 + M], oret[0:M, h * 128:(h + 1) * 128], ident_bf[0:M, 0:M],
                   [kor, "ident_bf"], ["psb"])
            cp("act", mix_dst, psb[:, 512:1024].rearrange("p (h n) -> p h n", h=4)[:, :, 0:M], ["psb"], [mix_key])

        def conv_a_post(l, accv, akeys, N, ybf_v, ysq_v, mix_dst, mix_key, tA, tB, tC, sbank, sbk):
            for ch in range(2):
                act(ybf_v[:, ch, :], accv[:, ch, :], AF.Identity, akeys + ["colA"], ["ybf"], bias=colA[:, l, 80 + ch:81 + ch])
                act(ysq_v[:, ch, :], accv[:, ch, :], AF.Square, akeys + ["colA"], ["ysq"], bias=colA[:, l, 80 + ch:81 + ch])
            yield
            for ch in range(2):
                mm(sbank[:, 0:N], ones_c[:], ybf_v[:, ch, :], ch == 0, ch == 1, ["ybf", "ones"], [sbk], inc=False)
            for ch in range(2):
                mm(sbank[:, 256:256 + N], ones_c[:], ysq_v[:, ch, :], ch == 0, ch == 1, ["ysq", "ones"], [sbk],
                   inc=(ch == 1))
            yield
            cp("dve", tA, sbank[:, 0:N], [sbk], ["tA"])
            tt("dve", tB, tA, tA, ALU.mult, ["tA"], ["tB"])
            tt("dve", tB, sbank[:, 256:256 + N], tB, ALU.subtract, [sbk, "tB"], ["tB"])
            ts("dve", tB, tB, 0.0, None, ALU.max, None, ["tB"], ["tB"])
            yield
            rsqrt_eps(tB, tB, 128, ["tB"], "tB")
            yield
            for ch in range(2):
                stt("dve", tC, accv[:, ch, :], colA[:, l, 80 + ch:81 + ch], tA, ALU.add, ALU.subtract,
                    akeys + ["colA", "tA"], ["tC"])
                tt("dve", tC, tC, tB, ALU.mult, ["tC", "tB"], ["tC"])
                act(mix_dst(ch), tC, AF.Silu, ["tC", "colA"], [mix_key],
                    bias=colA[:, l, 84 + ch:85 + ch], scale=colA[:, l, 82 + ch:83 + ch])

        def wout_post(l, mixv, key_mix, N, mov, key_mo, sqv, key_sq, rstd_ap, key_rstd, tmps_, key_tmp, col, is_tab, xv_):
            for oc in range(KC):
                pb = psf[oc % 2]
                pk = "ps%d" % (oc % 2)
                for k in range(KC):
                    mm(pb[:, 0:N], w_out[:, k, oc * 128:(oc + 1) * 128], mixv[:, k, :], k == 0, k == KC - 1,
                       ["w_out"] + (key_mix if isinstance(key_mix, list) else [key_mix]), [pk], inc=(k == KC - 1))
                cp("dve", mov[:, oc, :], pb[:, 0:N], [pk], [key_mo])
                act(sqv[:, oc, :], pb[:, 0:N], AF.Square, [pk], [key_sq])
            post_norm_res(mov, key_mo, sqv, key_sq, N, rstd_ap, key_rstd, tmps_, key_tmp, TB["G_m"], col, is_tab, xv_)

        def compute_mod(l):
            modT, tabA_m, tabG_m, tabA_f, tabG_f = modT_l[l], tabA_m_l[l], tabG_m_l[l], tabA_f_l[l], tabG_f_l[l]
            av = adaw_d[l].rearrange("(k p) n -> p k n", p=128)
            for pi in range(12):
                slot = ring[pi % RING]
                sk = "ring%d" % (pi % RING)
                sv = slot[:, 0:4096].rearrange("p (k n) -> p k n", k=KC)
                dma("pool", sk, sv, av[:, :, pi * 512:(pi + 1) * 512], (), [sk])
                pb = psf[pi % 2]
                pk = "ps%d" % (pi % 2)
                for jj in range(4):
                    for k in range(KC):
                        mm(pb[:, jj * 17:(jj + 1) * 17], sv[:, k, jj * 128:(jj + 1) * 128], scT17[:, k, :],
                           k == 0, k == KC - 1, [sk, "scT17"], [pk], inc=(k == KC - 1 and jj == 3))
                tt("dve", modT[:, pi * 4:(pi + 1) * 4, :], pb[:, 0:68].rearrange("p (j n) -> p j n", j=4),
                   colA[:, l, pi * 4:(pi + 1) * 4].unsqueeze(2).to_broadcast([128, 4, 17]), ALU.add,
                   [pk, "colA"], ["tab"])

            def gb(c0):
                return colA[:, l, c0:c0 + 8].unsqueeze(2).to_broadcast([128, 8, 17])
            ts("dve", tabA_m[:], modT[:, 8:16, :], 1.0, None, ALU.add, None, ["tab"], ["tab"])
            tt("dve", tabA_m[:], tabA_m[:], gb(48), ALU.mult, ["tab", "colA"], ["tab"])
            tt("dve", tabG_m[:], modT[:, 16:24, :], gb(56), ALU.mult, ["tab", "colA"], ["tab"])
            ts("dve", tabA_f[:], modT[:, 32:40, :], 1.0, None, ALU.add, None, ["tab"], ["tab"])
            tt("dve", tabA_f[:], tabA_f[:], gb(64), ALU.mult, ["tab", "colA"], ["tab"])
            tt("dve", tabG_f[:], modT[:, 40:48, :], gb(72), ALU.mult, ["tab", "colA"], ["tab"])


        import os as _os2
        SUB = int(_os2.environ.get("MK_SUB", "99"))

        def merge(gens, delays=None):
            gens = list(gens)
            delays = list(delays) if delays else [0] * len(gens)
            rnd = 0
            while gens:
                for i in range(len(gens) - 1, -1, -1):
                    pass
                for g, d in list(zip(gens, delays)):
                    if rnd < d:
                        continue
                    try:
                        next(g)
                    except StopIteration:
                        k = gens.index(g)
                        gens.pop(k)
                        delays.pop(k)
                rnd += 1

        def phase1_tile(l, hf, ti):
            Sst, S_bf = TB["Sst"], TB["S_bf"]
            tok = slice(ti * T1, (ti + 1) * T1)
            xv_ = xT[:, :, tok]
            if SUB < 1:
                return
            bufXs = [bufX2, bufX]
            hX = bufXs[ti % 2]
            hXk = "bufX%d" % (ti % 2)
            if ti == 0:
                pre_norm(xv_, T1, sqA, "sqA", hX, hXk, pnf[:, 0:T1], "pn_rstd", [pnf[:, T1:2 * T1], pnf[:, 2 * T1:3 * T1]],
                         "pn_tmp", TB["A_m"], TB["B_m"], 0, False)
            gen_FM = fm_proj_g(l, hX, hXk, T1, lambda ch: u_ext[:, ch, 30:30 + T1], lambda ch: z_ext[:, ch, 2:2 + T1],
                               cbf, tcc, [tmpC, tmpD], "sig",
                               u_hook=lambda ch: cp("act", u_bf[:, ch, 30:30 + T1], u_ext[:, ch, 30:30 + T1],
                                                    ["u_ext"], ["u_bf"]))

            def gen_CC():
                for ch in range(2):
                    zc = tmpC if ch == 0 else tmpD
                    zk = "sig%d" % ch
                    ts("dve", zc, z_ext[:, ch, 0:T1], colA[:, l, 86 + ch:87 + ch], None, ALU.mult, None,
                       ["z_ext", "colA"], [zk])
                    yield
                    for j in (1, 2):
                        stt("dve", zc, z_ext[:, ch, j:j + T1], colA[:, l, 86 + 2 * j + ch:87 + 2 * j + ch], zc,
                            ALU.mult, ALU.add, ["z_ext", "colA", zk], [zk])
                        yield
                    tt("dve", bufY[:, 6 + ch, :], cbf[:, ch, :], zc, ALU.mult, ["cbf", zk], ["bufY_c"])
                    cp("dve", z_ext[:, ch, 0:2], z_ext[:, ch, T1:T1 + 2], ["z_ext"], ["z_ext"])
                    yield

            def gen_CA():
                for ch in range(2):
                    for j in range(31):
                        mm(psf[6][:, ch * T1:(ch + 1) * T1], diagW[:, 2 * j + ch, :], u_bf[:, ch, j:j + T1], j == 0, j == 30,
                           ["diagW0", "diagW1", "u_bf"], ["ps6"], inc=(j == 30 and ch == 1))
                        if j % 8 == 7:
                            yield
                for ch in range(2):
                    cp("act", u_bf[:, ch, 0:30], u_bf[:, ch, T1:T1 + 30], ["u_bf"], ["u_bf"])
                yield
                yield from conv_a_post(l, psf[6][:, :].rearrange("p (c n) -> p c n", c=2), ["ps6"], T1, ybf, ysq,
                                       lambda ch: bufY[:, 4 + ch, :], "bufY_a", tmpA, tmpB, rstd1, psf[1], "ps1")

            SETS = [
                dict(rope_r=rope_r, rope_t=rope_t, sg=sg, on=on, qt_bf=qt_bf, k_bf=k_bf, kt_bf=kt_bf, v_bf=v_bf, qkT=qkT,
                     kTm=kTm, scT=scT, oret=oret, lnst=lnst, lnmv=lnmv, lnr=lnr, lnb=lnb, sfx="", banks=(2, 3, 4)),
            ]
            def gen_R(cc_):
                B = SETS[0]
                x = B["sfx"]
                bq, bv, bg = B["banks"]
                pq, pv, pg = "ps%d" % bq, "ps%d" % bv, "ps%d" % bg
                ci = hf * (NH // 128) + ti * (T1 // 128) + cc_
                S_cur, S_nxt = S_bf[0], S_bf[0]
                kS_cur, kS_nxt = "S_bf0", "S_bf0"
                tsl = slice(cc_ * 128, (cc_ + 1) * 128)
                tok_qkvg(l, lambda k: hX[:, k, tsl], hXk, 128, banks=(bq, bv, bg))
                yield
                cosv = cst[:, C_COS + ci * 32:C_COS + (ci + 1) * 32]
                sinv = cst[:, C_SIN + ci * 32:C_SIN + (ci + 1) * 32]
                cp("act", B["v_bf"], psf[bv][:, :], [pv], ["v_bf" + x])
                act(B["sg"], psf[bg][:, :], AF.Silu, [pg], ["sg" + x])
                rope(128, cosv, sinv, B["rope_r"], qb=bq, rt=B["rope_t"], x=x)
                yield
                rr = B["rope_r"]
                rq = rr[:, 0:256].rearrange("p (h d) -> p h d", h=4)
                rk = rr[:, 256:512].rearrange("p (h d) -> p h d", h=4)
                tt("dve", B["qt_bf"].rearrange("p (h d) -> p h d", h=4), rq,
                   cst[:, C_RDQ:C_RDQ + 4].unsqueeze(2).to_broadcast([128, 4, 64]), ALU.mult,
                   ["rope_r" + x, "rope_r" + x + "lo", "rope_r" + x + "hi", "cst"], ["qt_bf" + x])
                act(B["k_bf"], rr[:, 256:512], AF.Identity, ["rope_r" + x, "rope_r" + x + "lo", "rope_r" + x + "hi"], ["k_bf" + x], scale=0.125)
                tt("dve", B["kt_bf"].rearrange("p (h d) -> p h d", h=4), rk,
                   cst[:, C_UPD:C_UPD + 4].unsqueeze(2).to_broadcast([128, 4, 64]), ALU.mult,
                   ["rope_r" + x, "rope_r" + x + "lo", "rope_r" + x + "hi", "cst"], ["kt_bf" + x])
                yield
                tr(psb[:, 0:128], B["qt_bf"][:, 0:128], ident_bf[:], ["qt_bf" + x, "ident_bf"], ["psb"])
                tr(psb[:, 128:256], B["qt_bf"][:, 128:256], ident_bf[:], ["qt_bf" + x, "ident_bf"], ["psb"])
                tr(psb[:, 256:384], B["k_bf"][:, 0:128], ident_bf[:], ["k_bf" + x, "ident_bf"], ["psb"])
                tr(psb[:, 384:512], B["k_bf"][:, 128:256], ident_bf[:], ["k_bf" + x, "ident_bf"], ["psb"])
                cp("dve", B["qkT"], psb[:, 0:256], ["psb"], ["qkT" + x])
                kv = B["kTm"].rearrange("p (a b n) -> p a b n", a=2, b=2)
                cp("dve", kv[0:64, :, 0, :], psb[0:64, 256:512].rearrange("p (a n) -> p a n", a=2), ["psb"], ["kTm" + x])
                cp("act", kv[64:128, :, 1, :], psb[64:128, 256:512].rearrange("p (a n) -> p a n", a=2), ["psb"], ["kTm" + x])
                yield
                for h in range(4):
                    c0 = (h // 2) * 128
                    mm(psf[bg][:, h * 128:(h + 1) * 128], B["kTm"][:, h * 128:(h + 1) * 128],
                       B["qkT"][:, c0:c0 + 128], True, True, ["qkT" + x, "kTm" + x], [pg], inc=(h == 3))
                tt("dve", B["scT"], psf[bg][:, :], cst[:, C_DMT:C_DMT + 512], ALU.mult, [pg, "cst"], ["scT" + x])
                tt("dve", B["sg"], B["sg"], gn_bc[:, 0, :], ALU.mult, ["sg" + x, "gn_bc"], ["sg" + x])
                yield
                for h in range(4):
                    c0 = (h // 2) * 128
                    mm(psf[bv][:, h * 128:(h + 1) * 128], B["scT"][:, h * 128:(h + 1) * 128],
                       B["v_bf"][:, h * 128:(h + 1) * 128], True, False, ["scT" + x, "v_bf" + x], [pv])
                    mm(psf[bv][:, h * 128:(h + 1) * 128], B["qkT"][:, c0:c0 + 128], S_cur[:, h, :],
                       False, True, ["qkT" + x, kS_cur, kS_cur + "e", kS_cur + "o"], [pv], inc=(h == 3))
                yield
                BS = dict(lnst=B["lnst"], lnmv=B["lnmv"], lnr=B["lnr"], lnb=B["lnb"], on=B["on"], sg=B["sg"],
                          oret=B["oret"], sfx=x)
                if x == "_1":
                    BS["oret_key"] = "scT_1"
                yield from _ret_ln_gate(l, 128, psf[bv][:, :], [pv], bufY[:, 0:4, tsl], "bufY_r%d" % (cc_ % 2), BS)
                for h in range(4):
                    c0 = (h // 2) * 128
                    mm(psf[bq][:, h * 128:(h + 1) * 128], B["kt_bf"][:, c0:c0 + 128], B["v_bf"][:, h * 128:(h + 1) * 128],
                       True, True, ["kt_bf" + x, "v_bf" + x], [pq], inc=(h == 3))
                yield
                for h in range(4):
                    p0 = (h % 2) * 64
                    stt("dve", Sst[p0:p0 + 64, h // 2, :], Sst[p0:p0 + 64, h // 2, :], chunk_dec[h],
                        psf[bq][p0:p0 + 64, h * 128:(h + 1) * 128], ALU.mult, ALU.add, [pq, "Sst%d" % h], ["Sst%d" % h])
                sbv = S_nxt[:].rearrange("p (a b) n -> p a b n", b=2)
                cp("dve", sbv[0:64, :, 0, :], Sst[0:64, :, :], ["Sst", "Sst0", "Sst2"], [kS_nxt + "e"])
                cp("act", sbv[64:128, :, 1, :], Sst[64:128, :, :], ["Sst", "Sst1", "Sst3"], [kS_nxt + "o"])
                yield

            def gen_Rseq():
                yield from gen_R(0)
                yield from gen_R(1)

            gl = [gen_Rseq(), gen_FM, gen_CA(), gen_CC()]
            dl = [0, 0, 5, 10]
            if ti + 1 < NH // T1:
                tokn = slice((ti + 1) * T1, (ti + 2) * T1)
                gl.append(pre_norm_g(xT[:, :, tokn], T1, sqA, "sqA", bufXs[(ti + 1) % 2], "bufX%d" % ((ti + 1) % 2),
                                     pnf[:, 0:T1], "pn_rstd", [pnf[:, T1:2 * T1], pnf[:, 2 * T1:3 * T1]], "pn_tmp",
                                     TB["A_m"], TB["B_m"], 0, False, bank=5))
                dl.append(20)
            merge(gl, dl)
            wout_post(l, bufY, ["bufY", "bufY_r0", "bufY_r1", "bufY_a", "bufY_c"], T1, mo, "mo", hX, hXk, rstd1, "rstd1",
                      [tmpA, tmpB], "tmp", 0, False, xv_)

        def phase1_sample(l):
            M = NS
            pre_norm(xsT, M, sqs, "sqs", hTs, "hTs", rstd_s, "rstd_s", tmps, "tmps", TB["A_m"], TB["B_m"], 1, True)
            sig_s = [RCF[:, 9536:9536 + M], RCF[:, 9552:9552 + M]]
            fm_proj(l, hTs, "hTs", M, lambda ch: u_s[:, ch, :], lambda ch: z_s[:, ch, :], cb_s, tcc_s, sig_s, "sigs")
            tok_qkvg(l, lambda k: hTs[:, k, :], "hTs", M)
            rope(M, cst[0:M, C_SCOS:C_SCOS + 32], cst[0:M, C_SSIN:C_SSIN + 32], rope_r[0:M, :])
            qs = RCF[0:M, 4096:4352]
            ks = RCF[0:M, 4352:4608]
            vs = RCF[0:M, 4608:5120]
            cp("dve", qs, rope_r[0:M, 0:256], ["rope_r", "rope_rlo", "rope_rhi"], ["qs"])
            act(ks, rope_r[0:M, 256:512], AF.Identity, ["rope_r", "rope_rlo", "rope_rhi"], ["ks"], scale=0.125)
            cp("act", vs, psf[3][0:M, :], ["ps3"], ["vs"])
            act(sg[0:M, :], psf[4][0:M, :], AF.Silu, ["ps4"], ["sg"])
            tt("dve", sg[0:M, :], sg[0:M, :], gn_bc[0:M, 0, :], ALU.mult, ["sg", "gn_bc"], ["sg"])
            dma("sp", "scr", scr_q, qs, ["qs"], ["scr_q"])
            dma("sp", "scr", scr_k, ks, ["ks"], ["scr_k"])
            vs3 = vs.rearrange("p (h e) -> p h e", h=4)
            dma("sp", "scr", scr_v[:, :, 0, :], vs3, ["vs"], ["scr_v"])
            dma("sp", "scr", scr_v[:, :, 1, :], vs3, ["vs"], ["scr_v"])
            S128 = RCF[:, 0:4096]
            q128 = RCF[:, 5120:5152]
            k128 = RCF[:, 5152:5184]
            v128 = RCF[:, 5184:5312]
            o128 = RCF[:, 5312:5440]
            dma("sp", "s128", S128, sret_d[l].rearrange("s h (a d) e -> (s h a) (d e)", a=2), (), ["S128"])
            dma("sp", "ld128", q128, scr_q.rearrange("s (x d) -> (s x) d", d=32), ["scr_q"], ["q128"])
            dma("sp", "ld128", k128, scr_k.rearrange("s (x d) -> (s x) d", d=32), ["scr_k"], ["k128"])
            dma("sp", "ld128", v128, scr_v.rearrange("s h a e -> (s h a) e"), ["scr_v"], ["v128"])
            xpT = sb_xpT
            stA = sb_stA
            for g4 in range(4):
                stA = sb_stA_l[g4]
                ka = "stA%d" % g4
                dma("sp", ka, stA[0:120, :], sca_d[l, 4 * g4:4 * g4 + 4].rearrange("s j c -> (s j) c"), (), [ka])
            for g4 in range(4):
                stA = sb_stA_l[g4]
                ka = "stA%d" % g4
                for s in range(4):
                    dma("sp", "cas", cas_d[l, 4 * g4 + s, 0:29, :], stA[s * 30 + 1:s * 30 + 30, :], [ka], ())
                for ch in range(2):
                    tr(psf[6][:, 0:120], stA[0:120, ch * 128:(ch + 1) * 128], ident[0:120, 0:120], [ka, "cst"], ["ps6"])
                    cp("dve", xpT[:, ch, 4 * g4:4 * g4 + 4, 0:30], psf[6][:, 0:120].rearrange("p (s j) -> p s j", s=4),
                       ["ps6"], ["xpT"])
            prod = sb_prod
            for ch in range(2):
                cp("dve", xpT[:, ch, :, 30], u_s[:, ch, :], ["u_ext"], ["xpT"])
                wv = colB[:, l, :].rearrange("p (j c) -> p c j", c=2)[:, ch, :]
                tt("dve", prod[:], xpT[:, ch, :, :], wv.unsqueeze(1).to_broadcast([128, NS, 31]), ALU.mult,
                   ["xpT", "colB"], ["prod"])
                P.op("dve", (lambda ch=ch: nc.vector.tensor_reduce(out=acc_s[:, ch, :], in_=prod[:], axis=AX.X, op=ALU.add)),
                     ["prod"], ["acc"])
            for ch in range(2):
                tr(psf[6][0:M, 256 + ch * 128:256 + (ch + 1) * 128], u_s[:, ch, :], ident, ["u_ext", "cst"], ["ps6"])
            u_tok = sb_utok
            cp("dve", u_tok[0:M, :], psf[6][0:M, 256:512], ["ps6"], ["u_tok"])
            dma("sp", "cas", cas_d[l, :, 29, :], u_tok[0:M, :], ["u_tok"], ())
            for _ in conv_a_post(l, acc_s, ["acc"], M, ybf_s, ysq_s, lambda ch: mixTs[:, 4 + ch, :], "mixTs",
                                 RCF[:, 9568:9568 + M], RCF[:, 9584:9584 + M], RCF[:, 9600:9600 + M], psf[6], "ps6"):
                pass
            stC = sb_stC
            dma("sp", "stC", stC[0:32, :], scc_d[l].rearrange("s j c -> (s j) c"), (), ["stC"])
            dma("sp", "stC1", stC[32:48, :], scc_d[l, :, 1, :], (), ["stC1"])
            dma("sp", "ccs", ccs_d[l, :, 0, :], stC[32:48, :], ["stC1"], ())
            xpC = sb_xpC
            for ch in range(2):
                tr(psf[6][:, 0:32], stC[0:32, ch * 128:(ch + 1) * 128], ident[0:32, 0:32], ["stC", "cst"], ["ps6"])
                cp("dve", xpC[:, ch, :, 0:2], psf[6][:, 0:32].rearrange("p (s j) -> p s j", s=NS), ["ps6"], ["xpC"])
                cp("dve", xpC[:, ch, :, 2], z_s[:, ch, :], ["z_ext"], ["xpC"])
                wc = colA[:, l, 86:92].rearrange("p (j c) -> p c j", c=2)[:, ch, :]
                tt("dve", prod[:, :, 0:3], xpC[:, ch, :, :], wc.unsqueeze(1).to_broadcast([128, NS, 3]), ALU.mult,
                   ["xpC", "colA", "prod"], ["prod"])
                zc = RCF[:, 9616:9616 + M]
                P.op("dve", (lambda zc=zc: nc.vector.tensor_reduce(out=zc, in_=prod[:, :, 0:3], axis=AX.X, op=ALU.add)),
                     ["prod"], ["zcs"])
                tt("dve", mixTs[:, 6 + ch, :], cb_s[:, ch, :], zc, ALU.mult, ["cbf", "zcs"], ["mixTs"])
            for ch in range(2):
                tr(psf[6][0:M, 256 + ch * 128:256 + (ch + 1) * 128], z_s[:, ch, :], ident, ["z_ext", "cst"], ["ps6"])
            cp("dve", u_tok[0:M, :], psf[6][0:M, 256:512], ["ps6"], ["u_tok"])
            dma("sp", "ccs", ccs_d[l, :, 1, :], u_tok[0:M, :], ["u_tok"], ())
            S3 = S128.rearrange("p (d e) -> p d e", d=32)
            ts("dve", S128, S128, cst[:, C_GS:C_GS + 1], None, ALU.mult, None, ["S128", "cst"], ["S128"])
            SD = ["S128"] + ["S128_%d" % d_ for d_ in range(32)]
            for d_ in range(32):
                stt("dve", S3[:, d_, :], v128, k128[:, d_:d_ + 1], S3[:, d_, :], ALU.mult, ALU.add,
                    ["S128", "k128", "v128"], ["S128_%d" % d_])
            dma("sp", "rs", rs_d[l].rearrange("s h (a d) e -> (s h a) (d e)", a=2), S128, SD, ())
            o128b = RCF[:, 5440:5568]
            ts("dve", o128, S3[:, 0, :], q128[:, 0:1], None, ALU.mult, None, ["S128", "S128_0", "q128"], ["o128"])
            ts("dve", o128b, S3[:, 1, :], q128[:, 1:2], None, ALU.mult, None, ["S128", "S128_1", "q128"], ["o128b"])
            for d_ in range(2, 32):
                oa, ok_ = (o128, "o128") if d_ % 2 == 0 else (o128b, "o128b")
                stt("dve", oa, S3[:, d_, :], q128[:, d_:d_ + 1], oa, ALU.mult, ALU.add,
                    ["S128", "S128_%d" % d_, "q128", ok_], [ok_])
            tt("dve", o128, o128, o128b, ALU.add, ["o128", "o128b"], ["o128"])
            dma("sp", "scr", scr_o, o128, ["o128"], ["scr_o"])
            o2 = RCF[0:M, 8000:9024]
            dma("sp", "ld128", o2, scr_o.rearrange("(s x) e -> s (x e)", s=NS), ["scr_o"], ["o2"])
            o2v = o2.rearrange("p (h a e) -> p h a e", h=4, a=2)
            osum = RCF[0:M, 9024:9536]
            tt("dve", osum.rearrange("p (h e) -> p h e", h=4), o2v[:, :, 0, :], o2v[:, :, 1, :], ALU.add, ["o2"], ["osum"])
            for _ in ret_ln_gate(l, M, osum, ["osum"], mixTs[:, 0:4, :], "mixTs"):
                pass
            wout_post(l, mixTs, "mixTs", M, mos, "mos", sqs, "sqs", rstd_s, "rstd_s", tmps, "tmps", 1, True, xsT)

        def bview(a, n):
            return RCB[:, 2 * a:2 * (a + n)].bitcast(F32)
        sb_xpT = bview(0, 992).rearrange("p (c s j) -> p c s j", c=2, s=NS)
        sb_stA = bview(992, 256)
        sb_stA_l = [bview(992, 256), bview(2352, 256), bview(2608, 256), bview(2864, 256)]
        sb_prod = bview(1248, 496).rearrange("p (s j) -> p s j", s=NS)
        sb_utok = bview(1744, 256)
        sb_stC = bview(2000, 256)
        sb_xpC = bview(2256, 96).rearrange("p (c s j) -> p c s j", c=2, s=NS)

        def phase2(l, hf_):
            w1v = w1_d[l].rearrange("(k p) n -> p k n", p=128)
            w3v = w3_d[l].rearrange("(k p) n -> p k n", p=128)
            w2v = w2_d[l].rearrange("(k p) n -> p k n", p=128)
            pcount = [0]

            def next_slot():
                i = pcount[0] % RING
                pcount[0] += 1
                return ring[i], "ring%d" % i

            for hf in [hf_]:
                groups = []
                for t2 in range(NH // T2):
                    tok = slice(t2 * T2, (t2 + 1) * T2)
                    loc = slice(t2 * T2, (t2 + 1) * T2)
                    groups.append(dict(N=T2, x=xT[:, :, tok], h=hT2[:, :, loc], hk="hT2_%d" % t2, hid=hid[:, :, loc],
                                       hidk="hid_%d" % t2, fo=f_out[:, :, loc], fok="fo_%d" % t2, sq=sq2, sqk="sq2",
                                       psq=hT2[:, :, loc], rstd=rstd2, rk="rstd2", tmps=p2tmp[0:2], tk="p2s",
                                       col=0, tab=False))
                if hf == 1:
                    groups.append(dict(N=NS, x=xsT, h=hT2s, hk="hT2s", hid=hids, hidk="hids", fo=fouts, fok="fouts",
                                       sq=sq2s, sqk="sq2s", psq=hT2s, rstd=rstd_s, rk="rstd_s", tmps=tmps, tk="tmps",
                                       col=1, tab=True))
                for g in groups:
                    pre_norm(g["x"], g["N"], g["sq"], g["sqk"], g["h"], g["hk"], g["rstd"], g["rk"], g["tmps"], g["tk"],
                             TB["A_f"], TB["B_f"], g["col"], g["tab"])
                nxt_d = (l + 1, hf_) if l + 1 < NL else ((0, hf_ + 1) if hf_ + 1 < T // NH else None)
                dq = diag_ops(nxt_d[0]) if nxt_d is not None else []
                if nxt_d is not None:
                    DIAG_DONE.add(nxt_d)
                cnt = 0
                for pc in range(FC // 2):
                    slot, sk = next_slot()
                    w1p = slot[:, 0:2048].rearrange("p (k n) -> p k n", k=KC)
                    w3p = slot[:, 2048:4096].rearrange("p (k n) -> p k n", k=KC)
                    dma("pool", sk, w1p, w1v[:, :, pc * 256:(pc + 1) * 256], (), [sk])
                    dma("pool", sk, w3p, w3v[:, :, pc * 256:(pc + 1) * 256], (), [sk])
                    for fi in range(2):
                        f = pc * 2 + fi
                        for g in groups:
                            N = g["N"]
                            pa = psf[(cnt % 2) * 2]
                            pb = psf[(cnt % 2) * 2 + 1]
                            pak = "ps%d" % ((cnt % 2) * 2)
                            pbk = "ps%d" % ((cnt % 2) * 2 + 1)
                            tp = p2tmp[cnt % 2]
                            tpk = "p2s%d" % (cnt % 2)
                            cnt += 1
                            for k in range(KC):
                                mm(pa[:, 0:N], w1p[:, k, fi * 128:(fi + 1) * 128], g["h"][:, k, :], k == 0, k == KC - 1,
                                   [sk, g["hk"], g["hk"] + "_%d" % k], [pak], inc=(k == KC - 1))
                            for k in range(KC):
                                mm(pb[:, 0:N], w3p[:, k, fi * 128:(fi + 1) * 128], g["h"][:, k, :], k == 0, k == KC - 1,
                                   [sk, g["hk"], g["hk"] + "_%d" % k], [pbk], inc=(k == KC - 1))
                            act(tp[:, 0:N], pa[:, 0:N], AF.Silu, [pak], [tpk])
                            tt("dve", g["hid"][:, f, :], tp[:, 0:N], pb[:, 0:N], ALU.mult, [tpk, pbk], [g["hidk"]])
                            for _ in range(2):
                                if dq:
                                    dq.pop(0)()
                while dq:
                    dq.pop(0)()
                cnt = 0
                for pw in range(8):
                    slot, sk = next_slot()
                    w2p = slot[:, 0:FC * 128].rearrange("p (k n) -> p k n", k=FC)
                    dma("pool", sk, w2p, w2v[:, :, pw * 128:(pw + 1) * 128], (), [sk])
                    for oi in range(1):
                        oc = pw
                        for g in groups:
                            N = g["N"]
                            pb = psf[4 + cnt % 2]
                            pk = "ps%d" % (4 + cnt % 2)
                            cnt += 1
                            for k in range(FC):
                                mm(pb[:, 0:N], w2p[:, k, oi * 128:(oi + 1) * 128], g["hid"][:, k, :], k == 0, k == FC - 1,
                                   [sk, g["hidk"]], [pk], inc=(k == FC - 1))
                            cp("dve", g["fo"][:, oc, :], pb[:, 0:N], [pk], [g["fok"]])
                            act(g["psq"][:, oc, :], pb[:, 0:N], AF.Square, [pk], [g["hk"]])
                nxt = (l + 1, hf_) if l + 1 < NL else ((0, hf_ + 1) if hf_ + 1 < T // NH else None)
                if nxt is not None:
                    issue_w_in(nxt[0], extra_w=["hid_0", "hid_1", "sq2"])
                    WIN_DONE.add(nxt)
                for g in groups:
                    post_norm_res(g["fo"], g["fok"], g["psq"], g["hk"], g["N"], g["rstd"], g["rk"], g["tmps"], g["tk"],
                                  TB["G_f"], g["col"], g["tab"], g["x"])

        import os as _os
        STOP = int(_os.environ.get("MK_STOP", "99"))
        NL = int(_os.environ.get("MK_NL", str(DEPTH)))
        xv = xp_d.rearrange("(c p) d -> c p d", p=128)
        yv = yp_d.rearrange("(c p) d -> c p d", p=128)

        def load_x_half(hf):
            for cl in range(NH // 128):
                ci = hf * (NH // 128) + cl
                st = xstage[cl % 2]
                sk = "xst%d" % (cl % 2)
                dma("sp", sk, st, xv[ci], (), [sk])
                for half in range(2):
                    pb = psf[half]
                    pk = "ps%d" % half
                    for j in range(4):
                        c = half * 4 + j
                        tr(pb[:, j * 128:(j + 1) * 128], st[:, c * 128:(c + 1) * 128], ident, [sk, "cst"], [pk])
                    e = "dve" if half == 0 else "act"
                    cp(e, xT[:, half * 4:half * 4 + 4, cl * 128:(cl + 1) * 128],
                       pb[:].rearrange("p (k n) -> p k n", k=4), [pk], ["xT"])

        def store_y_half(hf):
            for cl in range(NH // 128):
                ci = hf * (NH // 128) + cl
                st = xstage[cl % 2]
                sk = "xst%d" % (cl % 2)
                for half in range(2):
                    pb = psf[half]
                    pk = "ps%d" % half
                    for j in range(4):
                        c = half * 4 + j
                        tr(pb[:, j * 128:(j + 1) * 128], xT[:, c, cl * 128:(cl + 1) * 128], ident, ["xT", "cst"], [pk])
                    e = "dve" if half == 0 else "act"
                    cp(e, st[:, half * 512:(half + 1) * 512], pb[:, :], [pk], [sk])
                dma("sp", sk, yv[ci], st, [sk], ())

        WIN_DONE = set()
        DIAG_DONE = set()

        def diag_ops(l):
            ops = []
            for idx_ in range(62):
                if idx_ % 2 == 0:
                    ops.append(lambda idx_=idx_: ts("dve", diagW[:, idx_, :], ident_bf[:], colB[:, l, idx_:idx_ + 1], None,
                                                    ALU.mult, None, ["ident_bf", "colB"], ["diagW%d" % (idx_ % 2)]))
                else:
                    ops.append(lambda idx_=idx_: act(diagW[:, idx_, :], ident_bf[:], AF.Copy, ["ident_bf", "colB"],
                                                     ["diagW%d" % (idx_ % 2)], scale=colB[:, l, idx_:idx_ + 1]))
            return ops

        def issue_w_in(l, extra_w=()):
            wiv = win_d[l].rearrange("(k p) n -> p k n", p=128)
            for j in (0, 512, 1024, 1536, 2048, 2560):
                j1 = min(INW, j + 512)
                dma("pool", "w_in%d" % (j // 512), w_in[:, :, j:j1], wiv[:, :, j:j1], (),
                    ["w_in%d" % (j // 512)] + list(extra_w))

        def prologue(l, hf):
            set_layer(l)
            Sst, S_bf = TB["Sst"], TB["S_bf"]
            if (l, hf) not in WIN_DONE:
                issue_w_in(l)
            wov = wout_d[l].rearrange("(k p) n -> p k n", p=128)
            for j in range(0, D, 512):
                dma("pool", "w_out", w_out[:, :, j:j + 512], wov[:, :, j:j + 512], (), ["w_out"])
            dma("sp", "gn", gn_bc[:, 0, :], gn_d[l:l + 1, :].partition_broadcast(128), (), ["gn_bc"])
            if hf == 0:
                mset("dve", Sst[:], 0.0, ["Sst"])
                mset("dve", S_bf[0][:], 0.0, ["S_bf0"])
                mset("dve", u_bf[:, :, 0:30], 0.0, ["u_bf"])
                mset("dve", z_ext[:, :, 0:2], 0.0, ["z_ext"])
            else:
                cp("dve", u_bf[:, :, 0:30], save_u[:, l, :, :], ["save"], ["u_bf"])
                cp("dve", z_ext[:, :, 0:2], save_z[:, l, :, :], ["save"], ["z_ext"])
            mset("dve", kTm, 0.0, ["kTm"])
            if (l, hf) not in DIAG_DONE:
                for f_ in diag_ops(l):
                    f_()

        def prefetch_pn0(l):
            set_layer(l)
            pre_norm(xT[:, :, 0:T1], T1, sqA, "sqA", bufX2, "bufX0", pnf[:, 0:T1], "pn_rstd",
                     [pnf[:, T1:2 * T1], pnf[:, 2 * T1:3 * T1]], "pn_tmp", TB["A_m"], TB["B_m"], 0, False)

        SKIPW = ("d_w_in", "d_w_out", "d_gn")
        load_x_half(0)
        for l in range(NL):
            compute_mod(l)
        P.barrier()
        for hf in range(T // NH):
            if STOP >= 3:
                prologue(0, hf)
            if hf > 0:
                load_x_half(hf)
            P.barrier(skip=SKIPW)
            for l in range(NL):
                if STOP < 3:
                    continue
                if l > 0:
                    prologue(l, hf)
                set_layer(l)
                Sst, S_bf = TB["Sst"], TB["S_bf"]
                for ti in range((NH // T1) if STOP >= 4 else 1):
                    phase1_tile(l, hf, ti)
                if hf == 0:
                    cp("dve", save_u[:, l, :, :], u_bf[:, :, 0:30], ["u_bf"], ["save"])
                    cp("dve", save_z[:, l, :, :], z_ext[:, :, 0:2], ["z_ext"], ["save"])
                else:
                    rpv = rp_d[l].rearrange("(hp hh) d e -> hh d hp e", hh=2)
                    for hh in range(2):
                        dma("sp", "rp", rpv[hh], Sst[hh * 64:(hh + 1) * 64, :, :], ["Sst", "Sst0", "Sst1", "Sst2", "Sst3"], ())
                    for ch in range(2):
                        tr(psf[6][0:30, ch * 128:(ch + 1) * 128], u_ext[:, ch, T1:T1 + 30], ident, ["u_ext", "cst"], ["ps6"])
                    cp("dve", st30[0:30, :], psf[6][0:30, 0:256], ["ps6"], ["st30"])
                    dma("sp", "cap", cap_d[l], st30[0:30, :], ["st30"], ())
                    for ch in range(2):
                        tr(psf[6][0:2, 256 + ch * 128:256 + (ch + 1) * 128], z_ext[:, ch, 0:2], ident, ["z_ext", "cst"], ["ps6"])
                    cp("dve", st30[0:2, :], psf[6][0:2, 256:512], ["ps6", "st30"], ["st30"])
                    dma("sp", "ccp", ccp_d[l], st30[0:2, :], ["st30"], ())
                P.barrier()
                if STOP < 5:
                    continue
                if hf == 1:
                    phase1_sample(l)
                    P.barrier()
                if STOP < 6:
                    continue
                phase2(l, hf)
                P.barrier(skip=SKIPW)
            store_y_half(hf)
            P.barrier(skip=SKIPW)

        for half in range(2):
            for j in range(4):
                c = half * 4 + j
                tr(psf[2 + half][0:NS, j * 128:(j + 1) * 128], xsT[:, c, :], ident, ["xsT", "cst"], ["ps%d" % (2 + half)])
            cp("dve", RCF[0:NS, 4096 + half * 512:4096 + (half + 1) * 512], psf[2 + half][0:NS, :],
               ["ps%d" % (2 + half)], ["ysst"])
        dma("sp", "ys", ys_d, RCF[0:NS, 4096:5120], ["ysst"], ())

        P.emit(nc, es)
    return nc


_CACHE = {}


def kernel(x_prompt, x_sample, c_prompt, c_sample, state_ret, state_conv_a, state_conv_c,
           ada_w, ada_b, norm_pre_mix, norm_post_mix, norm_pre_ffn, norm_post_ffn,
           w_in, w_out, ret_gn_g, conv_a_w, conv_a_b, conv_a_ln_g, conv_a_ln_b,
           conv_c_w, ffn_w1, ffn_w3, ffn_w2):
    f = lambda a: np.ascontiguousarray(np.asarray(a, dtype=np.float32))
    x_prompt, x_sample, c_prompt, c_sample = f(x_prompt), f(x_sample), f(c_prompt), f(c_sample)
    state_ret, state_conv_a, state_conv_c = f(state_ret), f(state_conv_a), f(state_conv_c)
    consts, chunk_dec = _host_consts()
    if "nc" not in _CACHE:
        _CACHE["nc"] = build_program(chunk_dec)
    nc = _CACHE["nc"]
    vecA = np.concatenate([
        f(ada_b).reshape(DEPTH, 48, 128),
        f(norm_pre_mix).reshape(DEPTH, 8, 128), f(norm_post_mix).reshape(DEPTH, 8, 128),
        f(norm_pre_ffn).reshape(DEPTH, 8, 128), f(norm_post_ffn).reshape(DEPTH, 8, 128),
        f(conv_a_b).reshape(DEPTH, 2, 128), f(conv_a_ln_g).reshape(DEPTH, 2, 128),
        f(conv_a_ln_b).reshape(DEPTH, 2, 128), f(conv_c_w).reshape(DEPTH, 6, 128)], axis=1)
    vecA = np.ascontiguousarray(vecA)
    vecB = np.ascontiguousarray(f(conv_a_w).reshape(DEPTH, 62, 128))
    shared = {"ada_w": f(ada_w), "w_in": f(w_in), "w_out": f(w_out), "ffn_w1": f(ffn_w1), "ffn_w3": f(ffn_w3),
              "ffn_w2": f(ffn_w2), "vecA": vecA, "vecB": vecB, "gn": f(ret_gn_g), "consts": consts}
    in_maps = []
    for b in range(8):
        sl = slice(NS * b, NS * (b + 1))
        m = dict(shared)
        m["xp"] = x_prompt[b]
        m["xs"] = np.ascontiguousarray(x_sample[sl, 0, :])
        m["c17"] = np.ascontiguousarray(np.concatenate([c_prompt[b:b + 1], c_sample[sl]], axis=0))
        m["sret"] = np.ascontiguousarray(state_ret[:, sl])
        m["sca"] = np.ascontiguousarray(state_conv_a[:, sl])
        m["scc"] = np.ascontiguousarray(state_conv_c[:, sl])
        in_maps.append(m)
    import os as _os
    _n = int(_os.environ.get("MK_NCORES", "8"))
    res = run_bass_kernel_spmd(nc, in_maps[:_n], core_ids=list(range(_n)))
    R = list(res.results) + [res.results[0]] * (8 - _n)
    yp = np.stack([np.asarray(R[b]["yp"]) for b in range(8)], axis=0).astype(np.float32)
    ys = np.concatenate([np.asarray(R[b]["ys"]) for b in range(8)], axis=0).reshape(128, 1, D).astype(np.float32)
    rp = np.stack([np.asarray(R[b]["rp"]) for b in range(8)], axis=1).astype(np.float32)
    cap = np.stack([np.asarray(R[b]["cap"]) for b in range(8)], axis=1).astype(np.float32)
    ccp = np.stack([np.asarray(R[b]["ccp"]) for b in range(8)], axis=1).astype(np.float32)
    rs = np.concatenate([np.asarray(R[b]["rs"]) for b in range(8)], axis=1).astype(np.float32)
    cas = np.concatenate([np.asarray(R[b]["cas"]) for b in range(8)], axis=1).astype(np.float32)
    ccs = np.concatenate([np.asarray(R[b]["ccs"]) for b in range(8)], axis=1).astype(np.float32)
    return (yp, ys, rp, cap, ccp, rs, cas, ccs)
```

```python
import numpy as np
from contextlib import ExitStack
import concourse.bass as bass
import concourse.mybir as mybir
from concourse.bass_utils import run_bass_kernel_spmd

F32 = mybir.dt.float32
BF16 = mybir.dt.bfloat16
AF = mybir.ActivationFunctionType
ALU = mybir.AluOpType
AX = mybir.AxisListType

D = 1024
KC = 8
T = 2048
DEPTH = 2
NS = 16
DFF = 2816
FC = 22
INW = 2816
EPS = 1e-6
T1 = 256
NT1 = T // T1
T2 = 512
NH = 1024
RING = 2
SLOT = 4096

C_ID = 0
C_COS = 128
C_SIN = C_COS + 512
C_DMT = C_SIN + 512
C_RDQ = C_DMT + 512
C_UPD = C_RDQ + 4
C_GS = C_UPD + 4
C_SCOS = C_GS + 1
C_SSIN = C_SCOS + 32
C_EPS = C_SSIN + 32
NCC = C_EPS + 1


def _host_consts():
    c = np.zeros((128, NCC), np.float32)
    c[:, C_ID:C_ID + 128] = np.eye(128, dtype=np.float32)
    half = 32
    inv = (np.float32(10000.0) ** (-(np.arange(half, dtype=np.float32) / np.float32(half)))).astype(np.float32)
    pos = np.arange(T, dtype=np.float32)
    ang = (pos[:, None] * inv[None, :]).astype(np.float32)
    cos = np.cos(ang.astype(np.float64)).astype(np.float32)
    sin = np.sin(ang.astype(np.float64)).astype(np.float32)
    c[:, C_COS:C_COS + 512] = cos.reshape(16, 128, 32).transpose(1, 0, 2).reshape(128, 512)
    c[:, C_SIN:C_SIN + 512] = sin.reshape(16, 128, 32).transpose(1, 0, 2).reshape(128, 512)
    angs = (np.float32(16384.0) * inv).astype(np.float32)
    c[:, C_SCOS:C_SCOS + 32] = np.cos(angs.astype(np.float64)).astype(np.float32)[None, :]
    c[:, C_SSIN:C_SSIN + 32] = np.sin(angs.astype(np.float64)).astype(np.float32)[None, :]
    hh = np.arange(4, dtype=np.float64)
    g = 1.0 - np.exp2(-5.0 - hh)
    lg = np.log(g)
    idx = np.arange(128, dtype=np.float64)
    m = idx[:, None, None]
    n = idx[None, None, :]
    dm = np.where(n >= m, np.exp((-m - 1.0) * lg[None, :, None]), 0.0)
    c[:, C_DMT:C_DMT + 512] = dm.reshape(128, 512).astype(np.float32)
    c[:, C_RDQ:C_RDQ + 4] = np.exp((idx[:, None] + 1.0) * lg[None, :]).astype(np.float32)
    c[:, C_UPD:C_UPD + 4] = (0.125 * np.exp((127.0 - idx[:, None]) * lg[None, :])).astype(np.float32)
    pidx = np.arange(128)
    c[:, C_GS] = g[(pidx // 2) % 4].astype(np.float32)
    c[:, C_EPS] = np.float32(EPS)
    chunk_dec = np.exp(128.0 * lg)
    return c, [float(x) for x in chunk_dec]


class Prog:
    ENG = ("pe", "dve", "act", "pool", "sp")

    def __init__(self):
        self.cnt = {e: 0 for e in self.ENG}
        self.dcnt = {}
        self.seen = {e: {} for e in self.ENG}
        self.lastw = {}
        self.reads = {}
        self.plan = {e: [] for e in self.ENG}

    def _waits(self, e, r, w):
        need = {}
        for k in r:
            t = self.lastw.get(k)
            if t:
                need[t[0]] = max(need.get(t[0], 0), t[1])
        for k in w:
            t = self.lastw.get(k)
            if t:
                need[t[0]] = max(need.get(t[0], 0), t[1])
            for nm, v in self.reads.get(k, {}).items():
                need[nm] = max(need.get(nm, 0), v)
        out = []
        for nm, v in need.items():
            if e == "pe" and nm == "s_pe":
                continue
            if self.seen[e].get(nm, 0) < v:
                self.seen[e][nm] = v
                out.append((nm, v))
        return out

    def _record(self, tok, r, w):
        for k in r:
            d = self.reads.setdefault(k, {})
            d[tok[0]] = max(d.get(tok[0], 0), tok[1])
        for k in w:
            self.lastw[k] = tok
            self.reads[k] = {}

    def op(self, e, fn, r=(), w=(), inc=True):
        pr = [k for k in r if k.startswith("ps")]
        if pr:
            r = [k for k in r if not k.startswith("ps")]
            w = list(w) + pr
        waits = self._waits(e, r, w)
        nm = "s_" + e
        if inc:
            self.cnt[e] += 1
            tok = (nm, self.cnt[e])
        else:
            tok = (nm, self.cnt[e] + 1)
        self._record(tok, r, w)
        self.plan[e].append((waits, fn, (nm, 1) if inc else None))

    def dma(self, q, chan, fn, r=(), w=()):
        waits = self._waits(q, r, w)
        nm = "d_" + chan
        self.dcnt[nm] = self.dcnt.get(nm, 0) + 16
        tok = (nm, self.dcnt[nm])
        self._record(tok, r, w)
        self.plan[q].append((waits, fn, (nm, 16)))

    def _all_tokens(self):
        toks = [("s_" + e, self.cnt[e]) for e in self.ENG if self.cnt[e] > 0]
        toks += list(self.dcnt.items())
        return toks

    def barrier(self, skip=()):
        toks = [t for t in self._all_tokens() if not t[0].startswith(tuple(skip))] if skip else self._all_tokens()
        for e in self.ENG:
            waits = []
            for nm, v in toks:
                if self.seen[e].get(nm, 0) < v:
                    self.seen[e][nm] = v
                    waits.append((nm, v))
            if waits:
                self.plan[e].append((waits, None, None))
        if skip:
            sk = tuple(skip)
            self.lastw = {k: v for k, v in self.lastw.items() if v[0].startswith(sk)}
            self.reads = {k: {n: c for n, c in d.items() if n.startswith(sk)} for k, d in self.reads.items()}
            self.reads = {k: d for k, d in self.reads.items() if d}
        else:
            self.lastw = {}
            self.reads = {}

    def emit(self, nc, es):
        toks = self._all_tokens()
        sems = {}
        for e in self.ENG:
            sems["s_" + e] = es.enter_context(nc.semaphore("s_" + e))
        for nm in self.dcnt:
            sems[nm] = es.enter_context(nc.semaphore(nm))
        blk = es.enter_context(nc.Block())
        engs = {"pe": nc.tensor, "dve": nc.vector, "act": nc.scalar, "pool": nc.gpsimd, "sp": nc.sync}

        def run(e):
            h = engs[e]
            for waits, fn, inc in self.plan[e]:
                for nm, v in waits:
                    h.wait_ge(sems[nm], v)
                if fn is not None:
                    ins = fn()
                    if inc is not None:
                        ins.then_inc(sems[inc[0]], inc[1])
            if e == "sp":
                for nm, v in toks:
                    h.wait_ge(sems[nm], v)

        @blk.sync
        def _(x):
            run("sp")

        @blk.scalar
        def _(x):
            run("act")

        @blk.vector
        def _(x):
            run("dve")

        @blk.gpsimd
        def _(x):
            run("pool")

        @blk.tensor
        def _(x):
            run("pe")


def build_program(chunk_dec):
    nc = bass.Bass("TRN2", target_bir_lowering=False)
    P = Prog()

    def din(name, shape, dt=F32):
        return nc.dram_tensor(name, list(shape), dt, kind="ExternalInput").ap()

    def dout(name, shape, dt=F32):
        return nc.dram_tensor(name, list(shape), dt, kind="ExternalOutput").ap()

    def dscr(name, shape, dt=F32):
        return nc.dram_tensor(name, list(shape), dt, kind="Internal").ap()

    xp_d = din("xp", [T, D])
    xs_d = din("xs", [NS, D])
    c17_d = din("c17", [17, D])
    sret_d = din("sret", [DEPTH, NS, 4, 64, 128])
    sca_d = din("sca", [DEPTH, NS, 30, 256])
    scc_d = din("scc", [DEPTH, NS, 2, 256])
    adaw_d = din("ada_w", [DEPTH, D, 6 * D])
    win_d = din("w_in", [DEPTH, D, INW])
    wout_d = din("w_out", [DEPTH, D, D])
    w1_d = din("ffn_w1", [DEPTH, D, DFF])
    w3_d = din("ffn_w3", [DEPTH, D, DFF])
    w2_d = din("ffn_w2", [DEPTH, DFF, D])
    vecA_d = din("vecA", [DEPTH, 92, 128])
    vecB_d = din("vecB", [DEPTH, 62, 128])
    gn_d = din("gn", [DEPTH, 512])
    cst_d = din("consts", [128, NCC])

    yp_d = dout("yp", [T, D])
    ys_d = dout("ys", [NS, D])
    rp_d = dout("rp", [DEPTH, 4, 64, 128])
    cap_d = dout("cap", [DEPTH, 30, 256])
    ccp_d = dout("ccp", [DEPTH, 2, 256])
    rs_d = dout("rs", [DEPTH, NS, 4, 64, 128])
    cas_d = dout("cas", [DEPTH, NS, 30, 256])
    ccs_d = dout("ccs", [DEPTH, NS, 2, 256])

    scr_q = dscr("scr_q", [NS, 256])
    scr_k = dscr("scr_k", [NS, 256])
    scr_v = dscr("scr_v", [NS, 4, 2, 128])
    scr_o = dscr("scr_o", [128, 128])

    es = ExitStack()
    with es:
        def sb(name, shape, dt=F32):
            return es.enter_context(nc.sbuf_tensor(name, list(shape), dt))

        xT = sb("xT", [128, KC, NH])
        RB = sb("RB", [128, 30720], BF16)
        w_in = RB[:, 0:22528].rearrange("p (k n) -> p k n", k=KC)
        w_out = RB[:, 22528:30720].rearrange("p (k n) -> p k n", k=KC)
        hid = RB[:, 0:22528].rearrange("p (f n) -> p f n", f=FC)
        hT2 = RB[:, 22528:30720].rearrange("p (k n) -> p k n", k=KC)
        RCF = sb("RCF", [128, 9728])
        RCB = sb("RCB", [128, RING * SLOT], BF16)
        f_out = RCF[:, 0:8192].rearrange("p (k n) -> p k n", k=KC)
        p2tmp = [RCF[:, 8192 + i * 512: 8192 + (i + 1) * 512] for i in range(2)]
        rstd2 = RCF[:, 9216:9728]
        ring = [RCB[:, i * SLOT:(i + 1) * SLOT] for i in range(RING)]
        sq2 = RB[:, 0:4096].rearrange("p (k n) -> p k n", k=KC)
        o = 0

        def cf(n):
            nonlocal o
            v = RCF[:, o:o + n]
            o += n
            return v
        mo = cf(KC * T1).rearrange("p (k n) -> p k n", k=KC)
        u_ext = cf(2 * (30 + T1)).rearrange("p (c n) -> p c n", c=2)
        z_ext = cf(2 * (2 + T1)).rearrange("p (c n) -> p c n", c=2)
        cbf = cf(2 * T1).rearrange("p (c n) -> p c n", c=2)
        acc = cf(2 * T1).rearrange("p (c n) -> p c n", c=2)
        tcc = cf(2 * T1).rearrange("p (c n) -> p c n", c=2)
        rstd1 = cf(T1)
        tmpA = cf(T1)
        tmpB = cf(T1)
        tmpC = cf(T1)
        tmpD = cf(T1)
        rope_r = cf(512)
        rope_t = cf(512)
        sg = cf(512)
        on = cf(512)
        assert o <= 9728, o
        xstage = [RCF[:, 0:1024], RCF[:, 1024:2048]]
        ob = 0

        def cb_(n):
            nonlocal ob
            v = RCB[:, ob:ob + n]
            ob += n
            return v
        bufX = cb_(KC * T1).rearrange("p (k n) -> p k n", k=KC)
        bufY = cb_(KC * T1).rearrange("p (k n) -> p k n", k=KC)
        ybf = cb_(2 * T1).rearrange("p (c n) -> p c n", c=2)
        ysq = cb_(2 * T1).rearrange("p (c n) -> p c n", c=2)
        qt_bf = cb_(256)
        k_bf = cb_(256)
        kt_bf = cb_(256)
        v_bf = cb_(512)
        qkT = cb_(256)
        kTm = cb_(512)
        scT = cb_(512)
        oret = cb_(512)
        assert ob <= RING * SLOT, ob

        cst = sb("cst", [128, NCC])
        ident = cst[:, C_ID:C_ID + 128]
        ident_bf = sb("ident_bf", [128, 128], BF16)
        ones_d = sb("ones_d", [128, 128], BF16)
        ones_c = sb("ones_c", [128, 128], BF16)
        colA = sb("colA", [128, DEPTH, 92])
        colB = sb("colB", [128, DEPTH, 62])
        gn_bc = sb("gn_bc", [128, 1, 512])
        vst = sb("vst", [128, 128])
        modT_l = [sb("modT%d" % i, [128, 48, 17]) for i in range(DEPTH)]
        tabA_m_l = [sb("tabA_m%d" % i, [128, KC, 17]) for i in range(DEPTH)]
        tabG_m_l = [sb("tabG_m%d" % i, [128, KC, 17]) for i in range(DEPTH)]
        tabA_f_l = [sb("tabA_f%d" % i, [128, KC, 17]) for i in range(DEPTH)]
        tabG_f_l = [sb("tabG_f%d" % i, [128, KC, 17]) for i in range(DEPTH)]
        TB = {}

        def set_layer(l):
            TB["modT"] = modT_l[l]
            TB["A_m"] = tabA_m_l[l]
            TB["G_m"] = tabG_m_l[l]
            TB["A_f"] = tabA_f_l[l]
            TB["G_f"] = tabG_f_l[l]
            TB["B_m"] = modT_l[l][:, 0:8, :]
            TB["B_f"] = modT_l[l][:, 24:32, :]
            TB["Sst"] = Sst_l[l]
            TB["S_bf"] = S_bf_l[l]
        scT17 = sb("scT17", [128, KC, 17], BF16)
        Sst_l = [sb("Sst%d" % i, [128, 2, 128]) for i in range(DEPTH)]
        S_bf_l = [[sb("S_bf%d_%d" % (i, q), [128, 4, 128], BF16) for q in range(2)] for i in range(DEPTH)]
        bufX2 = sb("bufX2", [128, KC, T1], BF16)
        sqA = sb("sqA", [128, KC, T1], BF16)
        pnf = sb("pnf", [128, 3 * T1])
        save_u = sb("save_u", [128, DEPTH, 2, 30], BF16)
        diagW = sb("diagW", [128, 62, 128], BF16)
        u_bf = sb("u_bf", [128, 2, 30 + T1], BF16)
        save_z = sb("save_z", [128, DEPTH, 2, 2])
        lnst = sb("lnst", [128, 4, 6])
        lnmv = sb("lnmv", [128, 4, 2])
        lnr = sb("lnr", [128, 4])
        lnb = sb("lnb", [128, 4])
        st30 = sb("st30", [32, 256])
        xsT = sb("xsT", [128, KC, NS])
        hTs = sb("hTs", [128, KC, NS], BF16)
        sqs = sb("sqs", [128, KC, NS], BF16)
        mixTs = sb("mixTs", [128, KC, NS], BF16)
        mos = sb("mos", [128, KC, NS])
        hT2s = sb("hT2s", [128, KC, NS], BF16)
        hids = sb("hids", [128, FC, NS], BF16)
        fouts = sb("fouts", [128, KC, NS])
        sq2s = sb("sq2s", [128, KC, NS], BF16)
        smallf = sb("smallf", [128, 12 * NS])
        u_s = smallf[:, 0:32].rearrange("p (c n) -> p c n", c=2)
        z_s = smallf[:, 32:64].rearrange("p (c n) -> p c n", c=2)
        cb_s = smallf[:, 64:96].rearrange("p (c n) -> p c n", c=2)
        acc_s = smallf[:, 96:128].rearrange("p (c n) -> p c n", c=2)
        tcc_s = smallf[:, 128:160].rearrange("p (c n) -> p c n", c=2)
        rstd_s = smallf[:, 160:176]
        tmps = [smallf[:, 176 + i * 16: 192 + i * 16] for i in range(1)]
        ybf_s = sb("ybf_s", [128, 2, NS], BF16)
        ysq_s = sb("ysq_s", [128, 2, NS], BF16)

        psf = [es.enter_context(nc.psum_tensor("ps%d" % i, [128, 512], F32)) for i in range(7)]
        psb = es.enter_context(nc.psum_tensor("psb", [128, 1024], BF16))

        def mm(out, lhsT, rhs, start, stop, r, w, inc=False):
            P.op("pe", lambda: nc.tensor.matmul(out, lhsT, rhs, start=start, stop=stop), r, w, inc)

        def tr(out, in_, idn, r, w):
            P.op("pe", lambda: nc.tensor.transpose(out, in_, idn), r, w, True)

        def act(out, in_, func, r, w, bias=None, scale=None):
            kw = {}
            if bias is not None:
                kw["bias"] = bias
            if scale is not None:
                kw["scale"] = scale
            P.op("act", lambda: nc.scalar.activation(out, in_, func, **kw), r, w)

        def veng(e):
            return nc.vector if e == "dve" else nc.gpsimd

        def tt(e, out, in0, in1, op, r, w):
            P.op(e, lambda: veng(e).tensor_tensor(out, in0, in1, op), r, w)

        def ts(e, out, in0, s1, s2, op0, op1, r, w):
            if s2 is None:
                P.op(e, lambda: veng(e).tensor_single_scalar(out, in0, s1, op0), r, w)
            else:
                P.op(e, lambda: veng(e).tensor_scalar(out, in0, s1, s2, op0, op1), r, w)

        def stt(e, out, in0, sc, in1, op0, op1, r, w):
            P.op(e, lambda: veng(e).scalar_tensor_tensor(out, in0, sc, in1, op0, op1), r, w)

        def cp(e, out, in_, r, w):
            if e == "act":
                P.op("act", lambda: nc.scalar.copy(out, in_), r, w)
            else:
                P.op(e, lambda: veng(e).tensor_copy(out, in_), r, w)

        def mset(e, out, val, w):
            P.op(e, lambda: veng(e).memset(out, val), (), w)

        def rsqrt_eps(out, in_, M, r, wkey):
            act(out, in_, AF.Ln, list(r) + ["cst"], [wkey], bias=cst[0:M, C_EPS:C_EPS + 1])
            act(out, out, AF.Exp, [wkey], [wkey], scale=-0.5)

        def dma(q, chan, out, in_, r, w):
            eng = {"sp": nc.sync, "pool": nc.gpsimd, "act": nc.scalar}[q]
            P.dma(q, chan, lambda: eng.dma_start(out=out, in_=in_), r, w)

        dma("sp", "cst", cst[:], cst_d, (), ["cst"])
        cp("dve", ident_bf[:], ident, ["cst"], ["ident_bf"])
        mset("dve", ones_d[:], 1.0 / 1024.0, ["ones"])
        mset("dve", ones_c[:], 1.0 / 256.0, ["ones"])
        for l in range(DEPTH):
            dma("sp", "vst", vst[0:92, :], vecA_d[l], (), ["vst"])
            tr(psf[6][:, 0:92], vst[0:92, :], ident[0:92, 0:92], ["vst", "cst"], ["ps6"])
            cp("dve", colA[:, l, :], psf[6][:, 0:92], ["ps6"], ["colA"])
            dma("sp", "vst", vst[0:62, :], vecB_d[l], (), ["vst"])
            tr(psf[6][:, 0:62], vst[0:62, :], ident[0:62, 0:62], ["vst", "cst"], ["ps6"])
            cp("dve", colB[:, l, :], psf[6][:, 0:62], ["ps6"], ["colB"])
        c17 = RCF[0:17, 2048:3072]
        dma("sp", "c17", c17, c17_d, (), ["c17"])
        act(c17, c17, AF.Silu, ["c17"], ["c17"])
        for c in range(KC):
            tr(psf[6][:, c * 17:(c + 1) * 17], c17[0:17, c * 128:(c + 1) * 128], ident[0:17, 0:17],
               ["c17", "cst"], ["ps6"])
        cp("dve", scT17[:].rearrange("p k n -> p (k n)"), psf[6][:, 0:KC * 17], ["ps6"], ["scT17"])

        dma("sp", "xst0", xstage[0][0:NS, :], xs_d, (), ["xst0"])
        for c in range(KC):
            tr(psf[0][:, c * NS:(c + 1) * NS], xstage[0][0:NS, c * 128:(c + 1) * 128], ident[0:NS, 0:NS],
               ["xst0", "cst"], ["ps0"])
        cp("dve", xsT[:].rearrange("p k n -> p (k n)"), psf[0][:, 0:KC * NS], ["ps0"], ["xsT"])
        P.barrier()

        XK = ["xT"] + ["xT%d" % k for k in range(KC)]
        def stats_rstd(sqv, N, rstd_ap, ones, key_sq, key_rstd, bank=6):
            bk = "ps%d" % bank
            for k in range(KC):
                mm(psf[bank][:, 0:N], ones[:], sqv[:, k, :], k == 0, k == KC - 1, [key_sq, key_sq + "_%d" % k, "ones"], [bk],
                   inc=(k == KC - 1))
            rsqrt_eps(rstd_ap, psf[bank][:, 0:N], 128, [bk], key_rstd)

        def pre_norm(*a, **kw):
            for _ in pre_norm_g(*a, **kw):
                pass

        def pre_norm_g(xv_, N, sqv, key_sq, hv, key_h, rstd_ap, key_rstd, tmps_, key_tmp, A, B, col, is_tab, bank=6):
            for k in range(KC):
                act(sqv[:, k, :], xv_[:, k, :], AF.Square, XK, [key_sq + "_%d" % k])
                if k % 2 == 1:
                    yield
            stats_rstd(sqv, N, rstd_ap, ones_d, key_sq, key_rstd, bank)
            yield
            for k in range(KC):
                if k % 2 == 0 and k > 0:
                    yield
                tp = tmps_[k % len(tmps_)]
                tk = key_tmp + str(k % len(tmps_))
                tt("dve", tp, xv_[:, k, :], rstd_ap, ALU.mult, XK + [key_rstd], [tk])
                if not is_tab:
                    act(hv[:, k, :], tp, AF.Identity, [tk, "tab"], [key_h + "_%d" % k],
                        bias=B[:, k, col:col + 1], scale=A[:, k, col:col + 1])
                else:
                    tt("dve", tp, tp, A[:, k, col:col + N], ALU.mult, [tk, "tab"], [tk])
                    tt("dve", hv[:, k, :], tp, B[:, k, col:col + N], ALU.add, [tk, "tab"], [key_h])

        def post_norm_res(mov, key_mo, sqv, key_sq, N, rstd_ap, key_rstd, tmps_, key_tmp, G, col, is_tab, xv_):
            stats_rstd(sqv, N, rstd_ap, ones_d, key_sq, key_rstd)
            for k in range(KC):
                tp = tmps_[k % len(tmps_)]
                tk = key_tmp + str(k % len(tmps_))
                tt("dve", tp, mov[:, k, :], rstd_ap, ALU.mult, [key_mo, key_rstd], [tk])
                if not is_tab:
                    stt("dve", xv_[:, k, :], tp, G[:, k, col:col + 1], xv_[:, k, :], ALU.mult, ALU.add,
                        [tk, "tab", "xT%d" % k], ["xT%d" % k])
                else:
                    tt("dve", tp, tp, G[:, k, col:col + N], ALU.mult, [tk, "tab"], [tk])
                    tt("dve", xv_[:, k, :], xv_[:, k, :], tp, ALU.add, [tk, "xT%d" % k], ["xT%d" % k])

        def fm_proj(*a, **kw):
            for _ in fm_proj_g(*a, **kw):
                pass

        def fm_proj_g(l, hv, key_h, N, u_dst, z_dst, cb_dst, tcc_v, sig_tmp, key_sig, u_hook=None):
            order = [2, 3, 0, 1, 6, 7, 8, 9, 4, 5]
            for idx_, j in enumerate(order):
                if idx_ > 0:
                    yield
                pb = psf[idx_ % 2]
                pk = "ps%d" % (idx_ % 2)
                col0 = 1536 + j * 128
                for k in range(KC):
                    mm(pb[:, 0:N], w_in[:, k, col0:col0 + 128], hv[:, k, :], k == 0, k == KC - 1,
                       ["w_in%d" % (col0 // 512), key_h, key_h + "_%d" % k], [pk], inc=(k == KC - 1))
                ch = j % 2
                if j in (2, 3):
                    act(sig_tmp[ch], pb[:, 0:N], AF.Tanh, [pk], [key_sig + str(ch)], scale=0.5)
                    ts("dve", sig_tmp[ch], sig_tmp[ch], 0.5, 0.5, ALU.mult, ALU.add, [key_sig + str(ch)], [key_sig + str(ch)])
                elif j in (0, 1):
                    tt("dve", u_dst(ch), pb[:, 0:N], sig_tmp[ch], ALU.mult, [pk, key_sig + str(ch)], ["u_ext"])
                    if u_hook is not None:
                        u_hook(ch)
                elif j in (6, 7):
                    cp("act", tcc_v[:, ch, :], pb[:, 0:N], [pk], ["tcc"])
                elif j in (8, 9):
                    tt("dve", z_dst(ch), pb[:, 0:N], tcc_v[:, ch, :], ALU.mult, [pk, "tcc"], ["z_ext"])
                else:
                    cp("act", cb_dst[:, ch, :], pb[:, 0:N], [pk], ["cbf"])

        def tok_qkvg(l, hv_chunk, key_h, M, banks=(2, 3, 4)):
            for gi in range(3):
                pb = psf[banks[gi]]
                pk = "ps%d" % banks[gi]
                for k in range(KC):
                    mm(pb[0:M, :], hv_chunk(k), w_in[:, k, gi * 512:(gi + 1) * 512], k == 0, k == KC - 1,
                       [key_h, key_h + "_%d" % k, "w_in%d" % gi], [pk], inc=(k == KC - 1))

        def rope(M, cosv, sinv, dst, qb=2, rt=None, x=""):
            if rt is None:
                rt = rope_t
            src = psf[qb][0:M, :].rearrange("p (h d) -> p h d", h=8)
            dv = dst.rearrange("p (h d) -> p h d", h=8)
            tv = rt[0:M, :].rearrange("p (h d) -> p h d", h=8)
            cb3 = cosv.unsqueeze(1).to_broadcast([M, 8, 32])
            sb3 = sinv.unsqueeze(1).to_broadcast([M, 8, 32])
            pq = "ps%d" % qb
            kr, kt_ = "rope_r" + x, "rope_t" + x
            tt("dve", dv[:, :, 0:32], src[:, :, 0:32], cb3, ALU.mult, [pq, "cst"], [kr + "lo"])
            tt("dve", tv[:, :, 0:32], src[:, :, 32:64], sb3, ALU.mult, [pq, "cst"], [kt_ + "lo"])
            tt("dve", dv[:, :, 32:64], src[:, :, 0:32], sb3, ALU.mult, [pq, "cst"], [kr + "hi"])
            tt("dve", tv[:, :, 32:64], src[:, :, 32:64], cb3, ALU.mult, [pq, "cst"], [kt_ + "hi"])
            tt("dve", dv[:, :, 0:32], dv[:, :, 0:32], tv[:, :, 0:32], ALU.subtract, [kr + "lo", kt_ + "lo"], [kr + "lo", kr])
            tt("dve", dv[:, :, 32:64], dv[:, :, 32:64], tv[:, :, 32:64], ALU.add, [kr + "hi", kt_ + "hi"], [kr + "hi", kr])

        def ret_ln_gate(l, M, osrc, okeys, mix_dst, mix_key, BS=None):
            if BS is None:
                BS = dict(lnst=lnst, lnmv=lnmv, lnr=lnr, lnb=lnb, on=on, sg=sg, oret=oret, sfx="")
            return _ret_ln_gate(l, M, osrc, okeys, mix_dst, mix_key, BS)

        def _ret_ln_gate(l, M, osrc, okeys, mix_dst, mix_key, BS):
            lnst, lnmv, lnr, lnb, on, sg, oret, x = (BS["lnst"], BS["lnmv"], BS["lnr"], BS["lnb"], BS["on"], BS["sg"],
                                                     BS["oret"], BS["sfx"])
            ov = osrc.rearrange("p (h e) -> p h e", h=4)
            kon = "on" + x if x != "_1" else "rope_t_1"
            kor = "oret" + x if x != "_1" else "scT_1"
            for h in range(4):
                P.op("dve", (lambda h=h: nc.vector.bn_stats(lnst[0:M, h, :], ov[:, h, :])), okeys, ["lnst%d" % h + x])
            for h in range(4):
                P.op("dve", (lambda h=h: nc.vector.bn_aggr(lnmv[0:M, h, :], lnst[0:M, h, :])), ["lnst%d" % h + x],
                     ["lnmv%d" % h + x])
            yield
            lmk = ["lnmv%d" % h + x for h in range(4)]
            rsqrt_eps(lnr[0:M, :], lnmv[0:M, :, 1], M, lmk, "lnr" + x)
            stt("dve", lnb[0:M, :], lnmv[0:M, :, 0], -1.0, lnr[0:M, :], ALU.mult, ALU.mult, lmk + ["lnr" + x], ["lnb" + x])
            yield
            onv = on[0:M, :].rearrange("p (h e) -> p h e", h=4)
            for h in range(4):
                act(onv[:, h, :], ov[:, h, :], AF.Identity, okeys + ["lnr" + x, "lnb" + x], [kon + "_%d" % h],
                    bias=lnb[0:M, h:h + 1], scale=lnr[0:M, h:h + 1])
            yield
            tt("dve", oret[0:M, :], on[0:M, :], sg[0:M, :], ALU.mult, [kon] + [kon + "_%d" % h for h in range(4)] + ["sg" + x], [kor])
            yield
            for h in range(4):
                tr(psb[:, 512 + h * 128: 512 + h * 128 + M], oret[0:M, h * 128:(h + 1) * 128], ident_bf[0:M, 0:M],
                   [kor, "ident_bf"], ["psb"])
            cp("act", mix_dst, psb[:, 512:1024].rearrange("p (h n) -> p h n", h=4)[:, :, 0:M], ["psb"], [mix_key])

        def conv_a_post(l, accv, akeys, N, ybf_v, ysq_v, mix_dst, mix_key, tA, tB, tC, sbank, sbk):
            for ch in range(2):
                act(ybf_v[:, ch, :], accv[:, ch, :], AF.Identity, akeys + ["colA"], ["ybf"], bias=colA[:, l, 80 + ch:81 + ch])
                act(ysq_v[:, ch, :], accv[:, ch, :], AF.Square, akeys + ["colA"], ["ysq"], bias=colA[:, l, 80 + ch:81 + ch])
            yield
            for ch in range(2):
                mm(sbank[:, 0:N], ones_c[:], ybf_v[:, ch, :], ch == 0, ch == 1, ["ybf", "ones"], [sbk], inc=False)
            for ch in range(2):
                mm(sbank[:, 256:256 + N], ones_c[:], ysq_v[:, ch, :], ch == 0, ch == 1, ["ysq", "ones"], [sbk],
                   inc=(ch == 1))
            yield
            cp("dve", tA, sbank[:, 0:N], [sbk], ["tA"])
            tt("dve", tB, tA, tA, ALU.mult, ["tA"], ["tB"])
            tt("dve", tB, sbank[:, 256:256 + N], tB, ALU.subtract, [sbk, "tB"], ["tB"])
            ts("dve", tB, tB, 0.0, None, ALU.max, None, ["tB"], ["tB"])
            yield
            rsqrt_eps(tB, tB, 128, ["tB"], "tB")
            yield
            for ch in range(2):
                stt("dve", tC, accv[:, ch, :], colA[:, l, 80 + ch:81 + ch], tA, ALU.add, ALU.subtract,
                    akeys + ["colA", "tA"], ["tC"])
                tt("dve", tC, tC, tB, ALU.mult, ["tC", "tB"], ["tC"])
                act(mix_dst(ch), tC, AF.Silu, ["tC", "colA"], [mix_key],
                    bias=colA[:, l, 84 + ch:85 + ch], scale=colA[:, l, 82 + ch:83 + ch])

        def wout_post(l, mixv, key_mix, N, mov, key_mo, sqv, key_sq, rstd_ap, key_rstd, tmps_, key_tmp, col, is_tab, xv_):
            for oc in range(KC):
                pb = psf[oc % 2]
                pk = "ps%d" % (oc % 2)
                for k in range(KC):
                    mm(pb[:, 0:N], w_out[:, k, oc * 128:(oc + 1) * 128], mixv[:, k, :], k == 0, k == KC - 1,
                       ["w_out"] + (key_mix if isinstance(key_mix, list) else [key_mix]), [pk], inc=(k == KC - 1))
                cp("dve", mov[:, oc, :], pb[:, 0:N], [pk], [key_mo])
                act(sqv[:, oc, :], pb[:, 0:N], AF.Square, [pk], [key_sq])
            post_norm_res(mov, key_mo, sqv, key_sq, N, rstd_ap, key_rstd, tmps_, key_tmp, TB["G_m"], col, is_tab, xv_)

        def compute_mod(l):
            modT, tabA_m, tabG_m, tabA_f, tabG_f = modT_l[l], tabA_m_l[l], tabG_m_l[l], tabA_f_l[l], tabG_f_l[l]
            av = adaw_d[l].rearrange("(k p) n -> p k n", p=128)
            for pi in range(12):
                slot = ring[pi % RING]
                sk = "ring%d" % (pi % RING)
                sv = slot[:, 0:4096].rearrange("p (k n) -> p k n", k=KC)
                dma("pool", sk, sv, av[:, :, pi * 512:(pi + 1) * 512], (), [sk])
                pb = psf[pi % 2]
                pk = "ps%d" % (pi % 2)
                for jj in range(4):
                    for k in range(KC):
                        mm(pb[:, jj * 17:(jj + 1) * 17], sv[:, k, jj * 128:(jj + 1) * 128], scT17[:, k, :],
                           k == 0, k == KC - 1, [sk, "scT17"], [pk], inc=(k == KC - 1 and jj == 3))
                tt("dve", modT[:, pi * 4:(pi + 1) * 4, :], pb[:, 0:68].rearrange("p (j n) -> p j n", j=4),
                   colA[:, l, pi * 4:(pi + 1) * 4].unsqueeze(2).to_broadcast([128, 4, 17]), ALU.add,
                   [pk, "colA"], ["tab"])

            def gb(c0):
                return colA[:, l, c0:c0 + 8].unsqueeze(2).to_broadcast([128, 8, 17])
            ts("dve", tabA_m[:], modT[:, 8:16, :], 1.0, None, ALU.add, None, ["tab"], ["tab"])
            tt("dve", tabA_m[:], tabA_m[:], gb(48), ALU.mult, ["tab", "colA"], ["tab"])
            tt("dve", tabG_m[:], modT[:, 16:24, :], gb(56), ALU.mult, ["tab", "colA"], ["tab"])
            ts("dve", tabA_f[:], modT[:, 32:40, :], 1.0, None, ALU.add, None, ["tab"], ["tab"])
            tt("dve", tabA_f[:], tabA_f[:], gb(64), ALU.mult, ["tab", "colA"], ["tab"])
            tt("dve", tabG_f[:], modT[:, 40:48, :], gb(72), ALU.mult, ["tab", "colA"], ["tab"])


        import os as _os2
        SUB = int(_os2.environ.get("MK_SUB", "99"))

        def merge(gens, delays=None):
            gens = list(gens)
            delays = list(delays) if delays else [0] * len(gens)
            rnd = 0
            while gens:
                for i in range(len(gens) - 1, -1, -1):
                    pass
                for g, d in list(zip(gens, delays)):
                    if rnd < d:
                        continue
                    try:
                        next(g)
                    except StopIteration:
                        k = gens.index(g)
                        gens.pop(k)
                        delays.pop(k)
                rnd += 1

        def phase1_tile(l, hf, ti):
            Sst, S_bf = TB["Sst"], TB["S_bf"]
            tok = slice(ti * T1, (ti + 1) * T1)
            xv_ = xT[:, :, tok]
            if SUB < 1:
                return
            bufXs = [bufX2, bufX]
            hX = bufXs[ti % 2]
            hXk = "bufX%d" % (ti % 2)
            if ti == 0:
                pre_norm(xv_, T1, sqA, "sqA", hX, hXk, pnf[:, 0:T1], "pn_rstd", [pnf[:, T1:2 * T1], pnf[:, 2 * T1:3 * T1]],
                         "pn_tmp", TB["A_m"], TB["B_m"], 0, False)
            gen_FM = fm_proj_g(l, hX, hXk, T1, lambda ch: u_ext[:, ch, 30:30 + T1], lambda ch: z_ext[:, ch, 2:2 + T1],
                               cbf, tcc, [tmpC, tmpD], "sig",
                               u_hook=lambda ch: cp("act", u_bf[:, ch, 30:30 + T1], u_ext[:, ch, 30:30 + T1],
                                                    ["u_ext"], ["u_bf"]))

            def gen_CC():
                for ch in range(2):
                    zc = tmpC if ch == 0 else tmpD
                    zk = "sig%d" % ch
                    ts("dve", zc, z_ext[:, ch, 0:T1], colA[:, l, 86 + ch:87 + ch], None, ALU.mult, None,
                       ["z_ext", "colA"], [zk])
                    yield
                    for j in (1, 2):
                        stt("dve", zc, z_ext[:, ch, j:j + T1], colA[:, l, 86 + 2 * j + ch:87 + 2 * j + ch], zc,
                            ALU.mult, ALU.add, ["z_ext", "colA", zk], [zk])
                        yield
                    tt("dve", bufY[:, 6 + ch, :], cbf[:, ch, :], zc, ALU.mult, ["cbf", zk], ["bufY_c"])
                    cp("dve", z_ext[:, ch, 0:2], z_ext[:, ch, T1:T1 + 2], ["z_ext"], ["z_ext"])
                    yield

            def gen_CA():
                for ch in range(2):
                    for j in range(31):
                        mm(psf[6][:, ch * T1:(ch + 1) * T1], diagW[:, 2 * j + ch, :], u_bf[:, ch, j:j + T1], j == 0, j == 30,
                           ["diagW0", "diagW1", "u_bf"], ["ps6"], inc=(j == 30 and ch == 1))
                        if j % 8 == 7:
                            yield
                for ch in range(2):
                    cp("act", u_bf[:, ch, 0:30], u_bf[:, ch, T1:T1 + 30], ["u_bf"], ["u_bf"])
                yield
                yield from conv_a_post(l, psf[6][:, :].rearrange("p (c n) -> p c n", c=2), ["ps6"], T1, ybf, ysq,
                                       lambda ch: bufY[:, 4 + ch, :], "bufY_a", tmpA, tmpB, rstd1, psf[1], "ps1")

            SETS = [
                dict(rope_r=rope_r, rope_t=rope_t, sg=sg, on=on, qt_bf=qt_bf, k_bf=k_bf, kt_bf=kt_bf, v_bf=v_bf, qkT=qkT,
                     kTm=kTm, scT=scT, oret=oret, lnst=lnst, lnmv=lnmv, lnr=lnr, lnb=lnb, sfx="", banks=(2, 3, 4)),
            ]
            def gen_R(cc_):
                B = SETS[0]
                x = B["sfx"]
                bq, bv, bg = B["banks"]
                pq, pv, pg = "ps%d" % bq, "ps%d" % bv, "ps%d" % bg
                ci = hf * (NH // 128) + ti * (T1 // 128) + cc_
                S_cur, S_nxt = S_bf[0], S_bf[0]
                kS_cur, kS_nxt = "S_bf0", "S_bf0"
                tsl = slice(cc_ * 128, (cc_ + 1) * 128)
                tok_qkvg(l, lambda k: hX[:, k, tsl], hXk, 128, banks=(bq, bv, bg))
                yield
                cosv = cst[:, C_COS + ci * 32:C_COS + (ci + 1) * 32]
                sinv = cst[:, C_SIN + ci * 32:C_SIN + (ci + 1) * 32]
                cp("act", B["v_bf"], psf[bv][:, :], [pv], ["v_bf" + x])
                act(B["sg"], psf[bg][:, :], AF.Silu, [pg], ["sg" + x])
                rope(128, cosv, sinv, B["rope_r"], qb=bq, rt=B["rope_t"], x=x)
                yield
                rr = B["rope_r"]
                rq = rr[:, 0:256].rearrange("p (h d) -> p h d", h=4)
                rk = rr[:, 256:512].rearrange("p (h d) -> p h d", h=4)
                tt("dve", B["qt_bf"].rearrange("p (h d) -> p h d", h=4), rq,
                   cst[:, C_RDQ:C_RDQ + 4].unsqueeze(2).to_broadcast([128, 4, 64]), ALU.mult,
                   ["rope_r" + x, "rope_r" + x + "lo", "rope_r" + x + "hi", "cst"], ["qt_bf" + x])
                act(B["k_bf"], rr[:, 256:512], AF.Identity, ["rope_r" + x, "rope_r" + x + "lo", "rope_r" + x + "hi"], ["k_bf" + x], scale=0.125)
                tt("dve", B["kt_bf"].rearrange("p (h d) -> p h d", h=4), rk,
                   cst[:, C_UPD:C_UPD + 4].unsqueeze(2).to_broadcast([128, 4, 64]), ALU.mult,
                   ["rope_r" + x, "rope_r" + x + "lo", "rope_r" + x + "hi", "cst"], ["kt_bf" + x])
                yield
                tr(psb[:, 0:128], B["qt_bf"][:, 0:128], ident_bf[:], ["qt_bf" + x, "ident_bf"], ["psb"])
                tr(psb[:, 128:256], B["qt_bf"][:, 128:256], ident_bf[:], ["qt_bf" + x, "ident_bf"], ["psb"])
                tr(psb[:, 256:384], B["k_bf"][:, 0:128], ident_bf[:], ["k_bf" + x, "ident_bf"], ["psb"])
                tr(psb[:, 384:512], B["k_bf"][:, 128:256], ident_bf[:], ["k_bf" + x, "ident_bf"], ["psb"])
                cp("dve", B["qkT"], psb[:, 0:256], ["psb"], ["qkT" + x])
                kv = B["kTm"].rearrange("p (a b n) -> p a b n", a=2, b=2)
                cp("dve", kv[0:64, :, 0, :], psb[0:64, 256:512].rearrange("p (a n) -> p a n", a=2), ["psb"], ["kTm" + x])
                cp("act", kv[64:128, :, 1, :], psb[64:128, 256:512].rearrange("p (a n) -> p a n", a=2), ["psb"], ["kTm" + x])
                yield
                for h in range(4):
                    c0 = (h // 2) * 128
                    mm(psf[bg][:, h * 128:(h + 1) * 128], B["kTm"][:, h * 128:(h + 1) * 128],
                       B["qkT"][:, c0:c0 + 128], True, True, ["qkT" + x, "kTm" + x], [pg], inc=(h == 3))
                tt("dve", B["scT"], psf[bg][:, :], cst[:, C_DMT:C_DMT + 512], ALU.mult, [pg, "cst"], ["scT" + x])
                tt("dve", B["sg"], B["sg"], gn_bc[:, 0, :], ALU.mult, ["sg" + x, "gn_bc"], ["sg" + x])
                yield
                for h in range(4):
                    c0 = (h // 2) * 128
                    mm(psf[bv][:, h * 128:(h + 1) * 128], B["scT"][:, h * 128:(h + 1) * 128],
                       B["v_bf"][:, h * 128:(h + 1) * 128], True, False, ["scT" + x, "v_bf" + x], [pv])
                    mm(psf[bv][:, h * 128:(h + 1) * 128], B["qkT"][:, c0:c0 + 128], S_cur[:, h, :],
                       False, True, ["qkT" + x, kS_cur, kS_cur + "e", kS_cur + "o"], [pv], inc=(h == 3))
                yield
                BS = dict(lnst=B["lnst"], lnmv=B["lnmv"], lnr=B["lnr"], lnb=B["lnb"], on=B["on"], sg=B["sg"],
                          oret=B["oret"], sfx=x)
                if x == "_1":
                    BS["oret_key"] = "scT_1"
                yield from _ret_ln_gate(l, 128, psf[bv][:, :], [pv], bufY[:, 0:4, tsl], "bufY_r%d" % (cc_ % 2), BS)
                for h in range(4):
                    c0 = (h // 2) * 128
                    mm(psf[bq][:, h * 128:(h + 1) * 128], B["kt_bf"][:, c0:c0 + 128], B["v_bf"][:, h * 128:(h + 1) * 128],
                       True, True, ["kt_bf" + x, "v_bf" + x], [pq], inc=(h == 3))
                yield
                for h in range(4):
                    p0 = (h % 2) * 64
                    stt("dve", Sst[p0:p0 + 64, h // 2, :], Sst[p0:p0 + 64, h // 2, :], chunk_dec[h],
                        psf[bq][p0:p0 + 64, h * 128:(h + 1) * 128], ALU.mult, ALU.add, [pq, "Sst%d" % h], ["Sst%d" % h])
                sbv = S_nxt[:].rearrange("p (a b) n -> p a b n", b=2)
                cp("dve", sbv[0:64, :, 0, :], Sst[0:64, :, :], ["Sst", "Sst0", "Sst2"], [kS_nxt + "e"])
                cp("act", sbv[64:128, :, 1, :], Sst[64:128, :, :], ["Sst", "Sst1", "Sst3"], [kS_nxt + "o"])
                yield

            def gen_Rseq():
                yield from gen_R(0)
                yield from gen_R(1)

            gl = [gen_Rseq(), gen_FM, gen_CA(), gen_CC()]
            dl = [0, 0, 5, 10]
            if ti + 1 < NH // T1:
                tokn = slice((ti + 1) * T1, (ti + 2) * T1)
                gl.append(pre_norm_g(xT[:, :, tokn], T1, sqA, "sqA", bufXs[(ti + 1) % 2], "bufX%d" % ((ti + 1) % 2),
                                     pnf[:, 0:T1], "pn_rstd", [pnf[:, T1:2 * T1], pnf[:, 2 * T1:3 * T1]], "pn_tmp",
                                     TB["A_m"], TB["B_m"], 0, False, bank=5))
                dl.append(6)
            merge(gl, dl)
            wout_post(l, bufY, ["bufY", "bufY_r0", "bufY_r1", "bufY_a", "bufY_c"], T1, mo, "mo", hX, hXk, rstd1, "rstd1",
                      [tmpA, tmpB], "tmp", 0, False, xv_)

        def phase1_sample(l):
            M = NS
            pre_norm(xsT, M, sqs, "sqs", hTs, "hTs", rstd_s, "rstd_s", tmps, "tmps", TB["A_m"], TB["B_m"], 1, True)
            sig_s = [RCF[:, 9536:9536 + M], RCF[:, 9552:9552 + M]]
            fm_proj(l, hTs, "hTs", M, lambda ch: u_s[:, ch, :], lambda ch: z_s[:, ch, :], cb_s, tcc_s, sig_s, "sigs")
            tok_qkvg(l, lambda k: hTs[:, k, :], "hTs", M)
            rope(M, cst[0:M, C_SCOS:C_SCOS + 32], cst[0:M, C_SSIN:C_SSIN + 32], rope_r[0:M, :])
            qs = RCF[0:M, 4096:4352]
            ks = RCF[0:M, 4352:4608]
            vs = RCF[0:M, 4608:5120]
            cp("dve", qs, rope_r[0:M, 0:256], ["rope_r", "rope_rlo", "rope_rhi"], ["qs"])
            act(ks, rope_r[0:M, 256:512], AF.Identity, ["rope_r", "rope_rlo", "rope_rhi"], ["ks"], scale=0.125)
            cp("act", vs, psf[3][0:M, :], ["ps3"], ["vs"])
            act(sg[0:M, :], psf[4][0:M, :], AF.Silu, ["ps4"], ["sg"])
            tt("dve", sg[0:M, :], sg[0:M, :], gn_bc[0:M, 0, :], ALU.mult, ["sg", "gn_bc"], ["sg"])
            dma("sp", "scr", scr_q, qs, ["qs"], ["scr_q"])
            dma("sp", "scr", scr_k, ks, ["ks"], ["scr_k"])
            vs3 = vs.rearrange("p (h e) -> p h e", h=4)
            dma("sp", "scr", scr_v[:, :, 0, :], vs3, ["vs"], ["scr_v"])
            dma("sp", "scr", scr_v[:, :, 1, :], vs3, ["vs"], ["scr_v"])
            S128 = RCF[:, 0:4096]
            q128 = RCF[:, 5120:5152]
            k128 = RCF[:, 5152:5184]
            v128 = RCF[:, 5184:5312]
            o128 = RCF[:, 5312:5440]
            dma("sp", "s128", S128, sret_d[l].rearrange("s h (a d) e -> (s h a) (d e)", a=2), (), ["S128"])
            dma("sp", "ld128", q128, scr_q.rearrange("s (x d) -> (s x) d", d=32), ["scr_q"], ["q128"])
            dma("sp", "ld128", k128, scr_k.rearrange("s (x d) -> (s x) d", d=32), ["scr_k"], ["k128"])
            dma("sp", "ld128", v128, scr_v.rearrange("s h a e -> (s h a) e"), ["scr_v"], ["v128"])
            xpT = sb_xpT
            stA = sb_stA
            for g4 in range(4):
                stA = sb_stA_l[g4]
                ka = "stA%d" % g4
                dma("sp", ka, stA[0:120, :], sca_d[l, 4 * g4:4 * g4 + 4].rearrange("s j c -> (s j) c"), (), [ka])
            for g4 in range(4):
                stA = sb_stA_l[g4]
                ka = "stA%d" % g4
                for s in range(4):
                    dma("sp", "cas", cas_d[l, 4 * g4 + s, 0:29, :], stA[s * 30 + 1:s * 30 + 30, :], [ka], ())
                for ch in range(2):
                    tr(psf[6][:, 0:120], stA[0:120, ch * 128:(ch + 1) * 128], ident[0:120, 0:120], [ka, "cst"], ["ps6"])
                    cp("dve", xpT[:, ch, 4 * g4:4 * g4 + 4, 0:30], psf[6][:, 0:120].rearrange("p (s j) -> p s j", s=4),
                       ["ps6"], ["xpT"])
            prod = sb_prod
            for ch in range(2):
                cp("dve", xpT[:, ch, :, 30], u_s[:, ch, :], ["u_ext"], ["xpT"])
                wv = colB[:, l, :].rearrange("p (j c) -> p c j", c=2)[:, ch, :]
                tt("dve", prod[:], xpT[:, ch, :, :], wv.unsqueeze(1).to_broadcast([128, NS, 31]), ALU.mult,
                   ["xpT", "colB"], ["prod"])
                P.op("dve", (lambda ch=ch: nc.vector.tensor_reduce(out=acc_s[:, ch, :], in_=prod[:], axis=AX.X, op=ALU.add)),
                     ["prod"], ["acc"])
            for ch in range(2):
                tr(psf[6][0:M, 256 + ch * 128:256 + (ch + 1) * 128], u_s[:, ch, :], ident, ["u_ext", "cst"], ["ps6"])
            u_tok = sb_utok
            cp("dve", u_tok[0:M, :], psf[6][0:M, 256:512], ["ps6"], ["u_tok"])
            dma("sp", "cas", cas_d[l, :, 29, :], u_tok[0:M, :], ["u_tok"], ())
            for _ in conv_a_post(l, acc_s, ["acc"], M, ybf_s, ysq_s, lambda ch: mixTs[:, 4 + ch, :], "mixTs",
                                 RCF[:, 9568:9568 + M], RCF[:, 9584:9584 + M], RCF[:, 9600:9600 + M], psf[6], "ps6"):
                pass
            stC = sb_stC
            dma("sp", "stC", stC[0:32, :], scc_d[l].rearrange("s j c -> (s j) c"), (), ["stC"])
            dma("sp", "stC1", stC[32:48, :], scc_d[l, :, 1, :], (), ["stC1"])
            dma("sp", "ccs", ccs_d[l, :, 0, :], stC[32:48, :], ["stC1"], ())
            xpC = sb_xpC
            for ch in range(2):
                tr(psf[6][:, 0:32], stC[0:32, ch * 128:(ch + 1) * 128], ident[0:32, 0:32], ["stC", "cst"], ["ps6"])
                cp("dve", xpC[:, ch, :, 0:2], psf[6][:, 0:32].rearrange("p (s j) -> p s j", s=NS), ["ps6"], ["xpC"])
                cp("dve", xpC[:, ch, :, 2], z_s[:, ch, :], ["z_ext"], ["xpC"])
                wc = colA[:, l, 86:92].rearrange("p (j c) -> p c j", c=2)[:, ch, :]
                tt("dve", prod[:, :, 0:3], xpC[:, ch, :, :], wc.unsqueeze(1).to_broadcast([128, NS, 3]), ALU.mult,
                   ["xpC", "colA", "prod"], ["prod"])
                zc = RCF[:, 9616:9616 + M]
                P.op("dve", (lambda zc=zc: nc.vector.tensor_reduce(out=zc, in_=prod[:, :, 0:3], axis=AX.X, op=ALU.add)),
                     ["prod"], ["zcs"])
                tt("dve", mixTs[:, 6 + ch, :], cb_s[:, ch, :], zc, ALU.mult, ["cbf", "zcs"], ["mixTs"])
            for ch in range(2):
                tr(psf[6][0:M, 256 + ch * 128:256 + (ch + 1) * 128], z_s[:, ch, :], ident, ["z_ext", "cst"], ["ps6"])
            cp("dve", u_tok[0:M, :], psf[6][0:M, 256:512], ["ps6"], ["u_tok"])
            dma("sp", "ccs", ccs_d[l, :, 1, :], u_tok[0:M, :], ["u_tok"], ())
            S3 = S128.rearrange("p (d e) -> p d e", d=32)
            ts("dve", S128, S128, cst[:, C_GS:C_GS + 1], None, ALU.mult, None, ["S128", "cst"], ["S128"])
            SD = ["S128"] + ["S128_%d" % d_ for d_ in range(32)]
            for d_ in range(32):
                stt("dve", S3[:, d_, :], v128, k128[:, d_:d_ + 1], S3[:, d_, :], ALU.mult, ALU.add,
                    ["S128", "k128", "v128"], ["S128_%d" % d_])
            dma("sp", "rs", rs_d[l].rearrange("s h (a d) e -> (s h a) (d e)", a=2), S128, SD, ())
            o128b = RCF[:, 5440:5568]
            ts("dve", o128, S3[:, 0, :], q128[:, 0:1], None, ALU.mult, None, ["S128", "S128_0", "q128"], ["o128"])
            ts("dve", o128b, S3[:, 1, :], q128[:, 1:2], None, ALU.mult, None, ["S128", "S128_1", "q128"], ["o128b"])
            for d_ in range(2, 32):
                oa, ok_ = (o128, "o128") if d_ % 2 == 0 else (o128b, "o128b")
                stt("dve", oa, S3[:, d_, :], q128[:, d_:d_ + 1], oa, ALU.mult, ALU.add,
                    ["S128", "S128_%d" % d_, "q128", ok_], [ok_])
            tt("dve", o128, o128, o128b, ALU.add, ["o128", "o128b"], ["o128"])
            dma("sp", "scr", scr_o, o128, ["o128"], ["scr_o"])
            o2 = RCF[0:M, 8000:9024]
            dma("sp", "ld128", o2, scr_o.rearrange("(s x) e -> s (x e)", s=NS), ["scr_o"], ["o2"])
            o2v = o2.rearrange("p (h a e) -> p h a e", h=4, a=2)
            osum = RCF[0:M, 9024:9536]
            tt("dve", osum.rearrange("p (h e) -> p h e", h=4), o2v[:, :, 0, :], o2v[:, :, 1, :], ALU.add, ["o2"], ["osum"])
            for _ in ret_ln_gate(l, M, osum, ["osum"], mixTs[:, 0:4, :], "mixTs"):
                pass
            wout_post(l, mixTs, "mixTs", M, mos, "mos", sqs, "sqs", rstd_s, "rstd_s", tmps, "tmps", 1, True, xsT)

        def bview(a, n):
            return RCB[:, 2 * a:2 * (a + n)].bitcast(F32)
        sb_xpT = bview(0, 992).rearrange("p (c s j) -> p c s j", c=2, s=NS)
        sb_stA = bview(992, 256)
        sb_stA_l = [bview(992, 256), bview(2352, 256), bview(2608, 256), bview(2864, 256)]
        sb_prod = bview(1248, 496).rearrange("p (s j) -> p s j", s=NS)
        sb_utok = bview(1744, 256)
        sb_stC = bview(2000, 256)
        sb_xpC = bview(2256, 96).rearrange("p (c s j) -> p c s j", c=2, s=NS)

        def phase2(l, hf_):
            w1v = w1_d[l].rearrange("(k p) n -> p k n", p=128)
            w3v = w3_d[l].rearrange("(k p) n -> p k n", p=128)
            w2v = w2_d[l].rearrange("(k p) n -> p k n", p=128)
            pcount = [0]

            def next_slot():
                i = pcount[0] % RING
                pcount[0] += 1
                return ring[i], "ring%d" % i

            for hf in [hf_]:
                groups = []
                for t2 in range(NH // T2):
                    tok = slice(t2 * T2, (t2 + 1) * T2)
                    loc = slice(t2 * T2, (t2 + 1) * T2)
                    groups.append(dict(N=T2, x=xT[:, :, tok], h=hT2[:, :, loc], hk="hT2_%d" % t2, hid=hid[:, :, loc],
                                       hidk="hid_%d" % t2, fo=f_out[:, :, loc], fok="fo_%d" % t2, sq=sq2, sqk="sq2",
                                       psq=hT2[:, :, loc], rstd=rstd2, rk="rstd2", tmps=p2tmp[0:2], tk="p2s",
                                       col=0, tab=False))
                if hf == 1:
                    groups.append(dict(N=NS, x=xsT, h=hT2s, hk="hT2s", hid=hids, hidk="hids", fo=fouts, fok="fouts",
                                       sq=sq2s, sqk="sq2s", psq=hT2s, rstd=rstd_s, rk="rstd_s", tmps=tmps, tk="tmps",
                                       col=1, tab=True))
                for g in groups:
                    pre_norm(g["x"], g["N"], g["sq"], g["sqk"], g["h"], g["hk"], g["rstd"], g["rk"], g["tmps"], g["tk"],
                             TB["A_f"], TB["B_f"], g["col"], g["tab"])
                nxt_d = (l + 1, hf_) if l + 1 < NL else ((0, hf_ + 1) if hf_ + 1 < T // NH else None)
                dq = diag_ops(nxt_d[0]) if nxt_d is not None else []
                if nxt_d is not None:
                    DIAG_DONE.add(nxt_d)
                cnt = 0
                for pc in range(FC // 2):
                    slot, sk = next_slot()
                    w1p = slot[:, 0:2048].rearrange("p (k n) -> p k n", k=KC)
                    w3p = slot[:, 2048:4096].rearrange("p (k n) -> p k n", k=KC)
                    dma("pool", sk, w1p, w1v[:, :, pc * 256:(pc + 1) * 256], (), [sk])
                    dma("pool", sk, w3p, w3v[:, :, pc * 256:(pc + 1) * 256], (), [sk])
                    for fi in range(2):
                        f = pc * 2 + fi
                        for g in groups:
                            N = g["N"]
                            pa = psf[(cnt % 2) * 2]
                            pb = psf[(cnt % 2) * 2 + 1]
                            pak = "ps%d" % ((cnt % 2) * 2)
                            pbk = "ps%d" % ((cnt % 2) * 2 + 1)
                            tp = p2tmp[cnt % 2]
                            tpk = "p2s%d" % (cnt % 2)
                            cnt += 1
                            for k in range(KC):
                                mm(pa[:, 0:N], w1p[:, k, fi * 128:(fi + 1) * 128], g["h"][:, k, :], k == 0, k == KC - 1,
                                   [sk, g["hk"], g["hk"] + "_%d" % k], [pak], inc=(k == KC - 1))
                            for k in range(KC):
                                mm(pb[:, 0:N], w3p[:, k, fi * 128:(fi + 1) * 128], g["h"][:, k, :], k == 0, k == KC - 1,
                                   [sk, g["hk"], g["hk"] + "_%d" % k], [pbk], inc=(k == KC - 1))
                            act(tp[:, 0:N], pa[:, 0:N], AF.Silu, [pak], [tpk])
                            tt("dve", g["hid"][:, f, :], tp[:, 0:N], pb[:, 0:N], ALU.mult, [tpk, pbk], [g["hidk"]])
                            for _ in range(2):
                                if dq:
                                    dq.pop(0)()
                while dq:
                    dq.pop(0)()
                cnt = 0
                for pw in range(8):
                    slot, sk = next_slot()
                    w2p = slot[:, 0:FC * 128].rearrange("p (k n) -> p k n", k=FC)
                    dma("pool", sk, w2p, w2v[:, :, pw * 128:(pw + 1) * 128], (), [sk])
                    for oi in range(1):
                        oc = pw
                        for g in groups:
                            N = g["N"]
                            pb = psf[4 + cnt % 2]
                            pk = "ps%d" % (4 + cnt % 2)
                            cnt += 1
                            for k in range(FC):
                                mm(pb[:, 0:N], w2p[:, k, oi * 128:(oi + 1) * 128], g["hid"][:, k, :], k == 0, k == FC - 1,
                                   [sk, g["hidk"]], [pk], inc=(k == FC - 1))
                            cp("dve", g["fo"][:, oc, :], pb[:, 0:N], [pk], [g["fok"]])
                            act(g["psq"][:, oc, :], pb[:, 0:N], AF.Square, [pk], [g["hk"]])
                nxt = (l + 1, hf_) if l + 1 < NL else ((0, hf_ + 1) if hf_ + 1 < T // NH else None)
                if nxt is not None:
                    issue_w_in(nxt[0], extra_w=["hid_0", "hid_1", "sq2"])
                    WIN_DONE.add(nxt)
                for g in groups:
                    post_norm_res(g["fo"], g["fok"], g["psq"], g["hk"], g["N"], g["rstd"], g["rk"], g["tmps"], g["tk"],
                                  TB["G_f"], g["col"], g["tab"], g["x"])

        import os as _os
        STOP = int(_os.environ.get("MK_STOP", "99"))
        NL = int(_os.environ.get("MK_NL", str(DEPTH)))
        xv = xp_d.rearrange("(c p) d -> c p d", p=128)
        yv = yp_d.rearrange("(c p) d -> c p d", p=128)

        def load_x_half(hf):
            for cl in range(NH // 128):
                ci = hf * (NH // 128) + cl
                st = xstage[cl % 2]
                sk = "xst%d" % (cl % 2)
                dma("sp", sk, st, xv[ci], (), [sk])
                for half in range(2):
                    pb = psf[half]
                    pk = "ps%d" % half
                    for j in range(4):
                        c = half * 4 + j
                        tr(pb[:, j * 128:(j + 1) * 128], st[:, c * 128:(c + 1) * 128], ident, [sk, "cst"], [pk])
                    e = "dve" if half == 0 else "act"
                    cp(e, xT[:, half * 4:half * 4 + 4, cl * 128:(cl + 1) * 128],
                       pb[:].rearrange("p (k n) -> p k n", k=4), [pk], ["xT"])

        def store_y_half(hf):
            for cl in range(NH // 128):
                ci = hf * (NH // 128) + cl
                st = xstage[cl % 2]
                sk = "xst%d" % (cl % 2)
                for half in range(2):
                    pb = psf[half]
                    pk = "ps%d" % half
                    for j in range(4):
                        c = half * 4 + j
                        tr(pb[:, j * 128:(j + 1) * 128], xT[:, c, cl * 128:(cl + 1) * 128], ident, ["xT", "cst"], [pk])
                    e = "dve" if half == 0 else "act"
                    cp(e, st[:, half * 512:(half + 1) * 512], pb[:, :], [pk], [sk])
                dma("sp", sk, yv[ci], st, [sk], ())

        WIN_DONE = set()
        DIAG_DONE = set()

        def diag_ops(l):
            ops = []
            for idx_ in range(62):
                if idx_ % 2 == 0:
                    ops.append(lambda idx_=idx_: ts("dve", diagW[:, idx_, :], ident_bf[:], colB[:, l, idx_:idx_ + 1], None,
                                                    ALU.mult, None, ["ident_bf", "colB"], ["diagW%d" % (idx_ % 2)]))
                else:
                    ops.append(lambda idx_=idx_: act(diagW[:, idx_, :], ident_bf[:], AF.Copy, ["ident_bf", "colB"],
                                                     ["diagW%d" % (idx_ % 2)], scale=colB[:, l, idx_:idx_ + 1]))
            return ops

        def issue_w_in(l, extra_w=()):
            wiv = win_d[l].rearrange("(k p) n -> p k n", p=128)
            for j in (0, 512, 1024, 1536, 2048, 2560):
                j1 = min(INW, j + 512)
                dma("pool", "w_in%d" % (j // 512), w_in[:, :, j:j1], wiv[:, :, j:j1], (),
                    ["w_in%d" % (j // 512)] + list(extra_w))

        def prologue(l, hf):
            set_layer(l)
            Sst, S_bf = TB["Sst"], TB["S_bf"]
            if (l, hf) not in WIN_DONE:
                issue_w_in(l)
            wov = wout_d[l].rearrange("(k p) n -> p k n", p=128)
            for j in range(0, D, 512):
                dma("pool", "w_out", w_out[:, :, j:j + 512], wov[:, :, j:j + 512], (), ["w_out"])
            dma("sp", "gn", gn_bc[:, 0, :], gn_d[l:l + 1, :].partition_broadcast(128), (), ["gn_bc"])
            if hf == 0:
                mset("dve", Sst[:], 0.0, ["Sst"])
                mset("dve", S_bf[0][:], 0.0, ["S_bf0"])
                mset("dve", u_bf[:, :, 0:30], 0.0, ["u_bf"])
                mset("dve", z_ext[:, :, 0:2], 0.0, ["z_ext"])
            else:
                cp("dve", u_bf[:, :, 0:30], save_u[:, l, :, :], ["save"], ["u_bf"])
                cp("dve", z_ext[:, :, 0:2], save_z[:, l, :, :], ["save"], ["z_ext"])
            mset("dve", kTm, 0.0, ["kTm"])
            if (l, hf) not in DIAG_DONE:
                for f_ in diag_ops(l):
                    f_()

        def prefetch_pn0(l):
            set_layer(l)
            pre_norm(xT[:, :, 0:T1], T1, sqA, "sqA", bufX2, "bufX0", pnf[:, 0:T1], "pn_rstd",
                     [pnf[:, T1:2 * T1], pnf[:, 2 * T1:3 * T1]], "pn_tmp", TB["A_m"], TB["B_m"], 0, False)

        SKIPW = ("d_w_in", "d_w_out", "d_gn")
        load_x_half(0)
        for l in range(NL):
            compute_mod(l)
        P.barrier()
        for hf in range(T // NH):
            if STOP >= 3:
                prologue(0, hf)
            if hf > 0:
                load_x_half(hf)
            P.barrier(skip=SKIPW)
            for l in range(NL):
                if STOP < 3:
                    continue
                if l > 0:
                    prologue(l, hf)
                set_layer(l)
                Sst, S_bf = TB["Sst"], TB["S_bf"]
                for ti in range((NH // T1) if STOP >= 4 else 1):
                    phase1_tile(l, hf, ti)
                if hf == 0:
                    cp("dve", save_u[:, l, :, :], u_bf[:, :, 0:30], ["u_bf"], ["save"])
                    cp("dve", save_z[:, l, :, :], z_ext[:, :, 0:2], ["z_ext"], ["save"])
                else:
                    rpv = rp_d[l].rearrange("(hp hh) d e -> hh d hp e", hh=2)
                    for hh in range(2):
                        dma("sp", "rp", rpv[hh], Sst[hh * 64:(hh + 1) * 64, :, :], ["Sst", "Sst0", "Sst1", "Sst2", "Sst3"], ())
                    for ch in range(2):
                        tr(psf[6][0:30, ch * 128:(ch + 1) * 128], u_ext[:, ch, T1:T1 + 30], ident, ["u_ext", "cst"], ["ps6"])
                    cp("dve", st30[0:30, :], psf[6][0:30, 0:256], ["ps6"], ["st30"])
                    dma("sp", "cap", cap_d[l], st30[0:30, :], ["st30"], ())
                    for ch in range(2):
                        tr(psf[6][0:2, 256 + ch * 128:256 + (ch + 1) * 128], z_ext[:, ch, 0:2], ident, ["z_ext", "cst"], ["ps6"])
                    cp("dve", st30[0:2, :], psf[6][0:2, 256:512], ["ps6", "st30"], ["st30"])
                    dma("sp", "ccp", ccp_d[l], st30[0:2, :], ["st30"], ())
                P.barrier()
                if STOP < 5:
                    continue
                if hf == 1:
                    phase1_sample(l)
                    P.barrier()
                if STOP < 6:
                    continue
                phase2(l, hf)
                P.barrier(skip=SKIPW)
            store_y_half(hf)
            P.barrier(skip=SKIPW)

        for half in range(2):
            for j in range(4):
                c = half * 4 + j
                tr(psf[2 + half][0:NS, j * 128:(j + 1) * 128], xsT[:, c, :], ident, ["xsT", "cst"], ["ps%d" % (2 + half)])
            cp("dve", RCF[0:NS, 4096 + half * 512:4096 + (half + 1) * 512], psf[2 + half][0:NS, :],
               ["ps%d" % (2 + half)], ["ysst"])
        dma("sp", "ys", ys_d, RCF[0:NS, 4096:5120], ["ysst"], ())

        P.emit(nc, es)
    return nc


_CACHE = {}


def kernel(x_prompt, x_sample, c_prompt, c_sample, state_ret, state_conv_a, state_conv_c,
           ada_w, ada_b, norm_pre_mix, norm_post_mix, norm_pre_ffn, norm_post_ffn,
           w_in, w_out, ret_gn_g, conv_a_w, conv_a_b, conv_a_ln_g, conv_a_ln_b,
           conv_c_w, ffn_w1, ffn_w3, ffn_w2):
    f = lambda a: np.ascontiguousarray(np.asarray(a, dtype=np.float32))
    x_prompt, x_sample, c_prompt, c_sample = f(x_prompt), f(x_sample), f(c_prompt), f(c_sample)
    state_ret, state_conv_a, state_conv_c = f(state_ret), f(state_conv_a), f(state_conv_c)
    consts, chunk_dec = _host_consts()
    if "nc" not in _CACHE:
        _CACHE["nc"] = build_program(chunk_dec)
    nc = _CACHE["nc"]
    vecA = np.concatenate([
        f(ada_b).reshape(DEPTH, 48, 128),
        f(norm_pre_mix).reshape(DEPTH, 8, 128), f(norm_post_mix).reshape(DEPTH, 8, 128),
        f(norm_pre_ffn).reshape(DEPTH, 8, 128), f(norm_post_ffn).reshape(DEPTH, 8, 128),
        f(conv_a_b).reshape(DEPTH, 2, 128), f(conv_a_ln_g).reshape(DEPTH, 2, 128),
        f(conv_a_ln_b).reshape(DEPTH, 2, 128), f(conv_c_w).reshape(DEPTH, 6, 128)], axis=1)
    vecA = np.ascontiguousarray(vecA)
    vecB = np.ascontiguousarray(f(conv_a_w).reshape(DEPTH, 62, 128))
    shared = {"ada_w": f(ada_w), "w_in": f(w_in), "w_out": f(w_out), "ffn_w1": f(ffn_w1), "ffn_w3": f(ffn_w3),
              "ffn_w2": f(ffn_w2), "vecA": vecA, "vecB": vecB, "gn": f(ret_gn_g), "consts": consts}
    in_maps = []
    for b in range(8):
        sl = slice(NS * b, NS * (b + 1))
        m = dict(shared)
        m["xp"] = x_prompt[b]
        m["xs"] = np.ascontiguousarray(x_sample[sl, 0, :])
        m["c17"] = np.ascontiguousarray(np.concatenate([c_prompt[b:b + 1], c_sample[sl]], axis=0))
        m["sret"] = np.ascontiguousarray(state_ret[:, sl])
        m["sca"] = np.ascontiguousarray(state_conv_a[:, sl])
        m["scc"] = np.ascontiguousarray(state_conv_c[:, sl])
        in_maps.append(m)
    import os as _os
    _n = int(_os.environ.get("MK_NCORES", "8"))
    res = run_bass_kernel_spmd(nc, in_maps[:_n], core_ids=list(range(_n)))
    R = list(res.results) + [res.results[0]] * (8 - _n)
    yp = np.stack([np.asarray(R[b]["yp"]) for b in range(8)], axis=0).astype(np.float32)
    ys = np.concatenate([np.asarray(R[b]["ys"]) for b in range(8)], axis=0).reshape(128, 1, D).astype(np.float32)
    rp = np.stack([np.asarray(R[b]["rp"]) for b in range(8)], axis=1).astype(np.float32)
    cap = np.stack([np.asarray(R[b]["cap"]) for b in range(8)], axis=1).astype(np.float32)
    ccp = np.stack([np.asarray(R[b]["ccp"]) for b in range(8)], axis=1).astype(np.float32)
    rs = np.concatenate([np.asarray(R[b]["rs"]) for b in range(8)], axis=1).astype(np.float32)
    cas = np.concatenate([np.asarray(R[b]["cas"]) for b in range(8)], axis=1).astype(np.float32)
    ccs = np.concatenate([np.asarray(R[b]["ccs"]) for b in range(8)], axis=1).astype(np.float32)
    return (yp, ys, rp, cap, ccp, rs, cas, ccs)
```
